# Optimizing a Trainium2 kernel written in Bass

```python
import jax
import jax.numpy as jnp
from jax import lax
import numpy as np

D_MODEL = 1024
BATCH = 2
SEQ = 16384
DEPTH = 4

FOX_HEADS = 8
FOX_HEAD_DIM = 64
MLA_HEADS = 8
MLA_Q_RANK = 256
MLA_KV_RANK = 128
MLA_NOPE_DIM = 64
MLA_ROPE_DIM = 32
MLA_V_DIM = 64
ROPE_THETA = 10000.0
DN_HEADS = 8
DN_HEAD_DIM = 64
DN_CONV_WIDTH = 4
DN_CHUNK = 64
D_FF = 2816
Q_BLOCK = 128
N_BRANCH = 3
NORM_EPS = 1e-6

FOX_WIDTH = FOX_HEADS * FOX_HEAD_DIM
MLA_QK_DIM = MLA_NOPE_DIM + MLA_ROPE_DIM
MLA_WIDTH = MLA_HEADS * MLA_V_DIM
DN_WIDTH = DN_HEADS * DN_HEAD_DIM
IN_WIDTHS = (FOX_WIDTH, FOX_WIDTH, FOX_WIDTH, FOX_HEADS,
             MLA_Q_RANK, MLA_KV_RANK, MLA_ROPE_DIM,
             DN_WIDTH, DN_WIDTH, DN_WIDTH, DN_WIDTH, DN_HEADS, DN_HEADS,
             N_BRANCH * D_MODEL)
N_IN = sum(IN_WIDTHS)

kernel_name = 'hybrid_fox_mla_gdn_macaron'


def rmsnorm(x, gain):
    xf = x.astype(jnp.float32)
    y = xf * lax.rsqrt(jnp.mean(xf * xf, axis=-1, keepdims=True) + NORM_EPS)
    return (y * gain.astype(jnp.float32)).astype(x.dtype)


def l2norm(x):
    xf = x.astype(jnp.float32)
    return xf * lax.rsqrt(jnp.sum(xf * xf, axis=-1, keepdims=True) + NORM_EPS)


def swiglu(h, w_gu, w_down):
    gate, up = jnp.split(h @ w_gu, 2, axis=-1)
    return (jax.nn.silu(gate) * up) @ w_down


def rope_tables(positions):
    half = MLA_ROPE_DIM // 2
    inv_freq = ROPE_THETA ** (-jnp.arange(half, dtype=jnp.float32) / half)
    ang = positions.astype(jnp.float32)[..., None] * inv_freq
    return jnp.cos(ang)[:, :, None, :], jnp.sin(ang)[:, :, None, :]


def apply_rope(x, cos, sin):
    x1, x2 = jnp.split(x.astype(jnp.float32), 2, axis=-1)
    return jnp.concatenate([x1 * cos - x2 * sin, x2 * cos + x1 * sin], axis=-1).astype(x.dtype)


def causal_short_conv(x, w):
    width = w.shape[0]
    s = x.shape[1]
    xp = jnp.pad(x, ((0, 0), (width - 1, 0), (0, 0)))
    return sum(xp[:, i:i + s] * w[i] for i in range(width))


def block_causal_attention(q, k, v, scale, log_decay_cum=None):
    b, s, h, dqk = q.shape
    dv = v.shape[-1]
    nb = s // Q_BLOCK
    q_blocks = q.reshape(b, nb, Q_BLOCK, h, dqk).transpose(1, 0, 3, 2, 4)
    k_pos = jnp.arange(s)
    xs = (jnp.arange(nb), q_blocks)
    if log_decay_cum is not None:
        c = log_decay_cum.astype(jnp.float32).transpose(0, 2, 1)
        c_blocks = c.reshape(b, h, nb, Q_BLOCK).transpose(2, 0, 1, 3)
        xs = (jnp.arange(nb), q_blocks, c_blocks)

    def one_block(args):
        idx, q_blk = args[0], args[1]
        logits = jnp.einsum('bhqd,bkhd->bhqk', q_blk, k).astype(jnp.float32) * scale
        if log_decay_cum is not None:
            logits = logits + args[2][..., :, None] - c[:, :, None, :]
        q_pos = idx * Q_BLOCK + jnp.arange(Q_BLOCK)
        logits = jnp.where(k_pos[None, :] <= q_pos[:, None], logits, -jnp.inf)
        p = jax.nn.softmax(logits, axis=-1).astype(v.dtype)
        return jnp.einsum('bhqk,bkhd->bqhd', p, v)

    out = lax.map(one_block, xs)
    return out.transpose(1, 0, 2, 3, 4).reshape(b, s, h, dv)


def chunk_gated_delta_rule(q, k, v, g, beta):
    b, s, h, dk = q.shape
    dv = v.shape[-1]
    nc = s // DN_CHUNK

    def to_chunks(t):
        return jnp.moveaxis(t.reshape((b, nc, DN_CHUNK, h) + t.shape[3:]), 3, 2)

    q, k, v, g, beta = (to_chunks(t) for t in (q, k, v, g, beta))
    gc = jnp.cumsum(g, axis=-1)
    idx = jnp.arange(DN_CHUNK)
    causal = idx[:, None] >= idx[None, :]
    strict = idx[:, None] > idx[None, :]
    decay_mat = jnp.exp(jnp.where(causal, gc[..., :, None] - gc[..., None, :], -jnp.inf))
    k_beta = k * beta[..., None]
    v_beta = v * beta[..., None]
    m = jnp.where(strict, jnp.einsum('bnhid,bnhjd->bnhij', k_beta, k) * decay_mat, 0.0)
    eye = jnp.eye(DN_CHUNK, dtype=jnp.float32)
    t_inv = lax.linalg.triangular_solve(eye + m, jnp.broadcast_to(eye, m.shape),
                                        left_side=True, lower=True, unit_diagonal=True)
    w = t_inv @ (k_beta * jnp.exp(gc)[..., None])
    u = t_inv @ v_beta
    qk = jnp.einsum('bnhid,bnhjd->bnhij', q, k) * decay_mat
    q_dec = q * jnp.exp(gc)[..., None]
    k_dec = k * jnp.exp(gc[..., -1:] - gc)[..., None]
    g_last = jnp.exp(gc[..., -1])

    def step(state, xs_c):
        w_c, u_c, qk_c, q_c, k_c, gl = xs_c
        v_new = u_c - w_c @ state
        o_c = q_c @ state + qk_c @ v_new
        state = state * gl[..., None, None] + jnp.einsum('bhcd,bhce->bhde', k_c, v_new)
        return state, o_c

    xs = tuple(jnp.moveaxis(t, 1, 0) for t in (w, u, qk, q_dec, k_dec, g_last))
    state0 = jnp.zeros((b, h, dk, dv), jnp.float32)
    _, o = lax.scan(step, state0, xs)
    return o.transpose(1, 0, 3, 2, 4).reshape(b, s, h, dv)


def hybrid_mixer(h, cos, sin, w_in, b_gate, fox_b_f, mla_q_norm, mla_w_uq, mla_kv_norm, mla_w_ukv,
                 dn_conv_w, dn_a_log, dn_dt_bias, dn_o_norm, w_br_fox, w_br_mla, w_br_dn, w_out):
    b, s, _ = h.shape
    f32 = jnp.float32
    splits = np.cumsum(IN_WIDTHS)[:-1]
    (fox_q, fox_k, fox_v, fox_f, mla_cq, mla_ckv, mla_kr,
     dn_q, dn_k, dn_v, dn_z, dn_b, dn_a, gate_logits) = jnp.split(h @ w_in, splits, axis=-1)

    fox_shape = (b, s, FOX_HEADS, FOX_HEAD_DIM)
    log_f = jax.nn.log_sigmoid(fox_f.astype(f32) + fox_b_f.astype(f32))
    fox_out = block_causal_attention(fox_q.reshape(fox_shape), fox_k.reshape(fox_shape),
                                     fox_v.reshape(fox_shape), FOX_HEAD_DIM ** -0.5,
                                     jnp.cumsum(log_f, axis=1))
    y_fox = fox_out.reshape(b, s, FOX_WIDTH) @ w_br_fox

    q = (rmsnorm(mla_cq, mla_q_norm) @ mla_w_uq).reshape(b, s, MLA_HEADS, MLA_QK_DIM)
    q_nope, q_rope = jnp.split(q, [MLA_NOPE_DIM], axis=-1)
    kv = (rmsnorm(mla_ckv, mla_kv_norm) @ mla_w_ukv).reshape(b, s, MLA_HEADS, MLA_NOPE_DIM + MLA_V_DIM)
    k_nope, mla_v = jnp.split(kv, [MLA_NOPE_DIM], axis=-1)
    k_rope = apply_rope(mla_kr[:, :, None, :], cos, sin)
    q_full = jnp.concatenate([q_nope, apply_rope(q_rope, cos, sin)], axis=-1)
    k_full = jnp.concatenate([k_nope, jnp.broadcast_to(k_rope, (b, s, MLA_HEADS, MLA_ROPE_DIM))], axis=-1)
    mla_out = block_causal_attention(q_full, k_full, mla_v, MLA_QK_DIM ** -0.5)
    y_mla = mla_out.reshape(b, s, MLA_WIDTH) @ w_br_mla

    qkv = jax.nn.silu(causal_short_conv(jnp.concatenate([dn_q, dn_k, dn_v], axis=-1), dn_conv_w))
    cq, ck, cv = jnp.split(qkv, 3, axis=-1)
    dn_shape = (b, s, DN_HEADS, DN_HEAD_DIM)
    q_dn = l2norm(cq.reshape(dn_shape)) * (DN_HEAD_DIM ** -0.5)
    k_dn = l2norm(ck.reshape(dn_shape))
    v_dn = cv.reshape(dn_shape).astype(f32)
    beta = jax.nn.sigmoid(dn_b.astype(f32))
    g = -jnp.exp(dn_a_log.astype(f32)) * jax.nn.softplus(dn_a.astype(f32) + dn_dt_bias.astype(f32))
    o = chunk_gated_delta_rule(q_dn, k_dn, v_dn, g, beta)
    o = rmsnorm(o, dn_o_norm).astype(h.dtype) * jax.nn.silu(dn_z.reshape(dn_shape))
    y_dn = o.reshape(b, s, DN_WIDTH) @ w_br_dn

    g_fox, g_mla, g_dn = jnp.split(jax.nn.sigmoid(gate_logits + b_gate), N_BRANCH, axis=-1)
    return (g_fox * y_fox + g_mla * y_mla + g_dn * y_dn) @ w_out


def setup_inputs(seed: int = 0) -> dict:
    key = jax.random.key(seed)
    ks = iter(list(jax.random.split(key, 32)))
    f32 = jnp.float32
    L = DEPTH

    def nrm(shape, scale):
        return jax.random.normal(next(ks), shape, f32) * scale

    def gain(shape):
        return 1.0 + nrm(shape, 0.1)

    x = jax.random.normal(next(ks), (BATCH, SEQ, D_MODEL), f32)
    positions = (jnp.arange(SEQ, dtype=jnp.int32)[None, :]
                 + jax.random.randint(next(ks), (BATCH, 1), 0, 1024, dtype=jnp.int32))
    ffn1_norm = gain((L, D_MODEL))
    ffn1_w_gu = nrm((L, D_MODEL, 2 * D_FF), D_MODEL ** -0.5)
    ffn1_w_down = nrm((L, D_FF, D_MODEL), D_FF ** -0.5)
    mix_norm = gain((L, D_MODEL))
    w_in = nrm((L, D_MODEL, N_IN), D_MODEL ** -0.5)
    b_gate = nrm((L, N_BRANCH * D_MODEL), 0.1)
    fox_b_f = 4.0 + nrm((L, FOX_HEADS), 0.5)
    mla_q_norm = gain((L, MLA_Q_RANK))
    mla_w_uq = nrm((L, MLA_Q_RANK, MLA_HEADS * MLA_QK_DIM), MLA_Q_RANK ** -0.5)
    mla_kv_norm = gain((L, MLA_KV_RANK))
    mla_w_ukv = nrm((L, MLA_KV_RANK, MLA_HEADS * (MLA_NOPE_DIM + MLA_V_DIM)), MLA_KV_RANK ** -0.5)
    dn_conv_w = nrm((L, DN_CONV_WIDTH, 3 * DN_WIDTH), DN_CONV_WIDTH ** -0.5)
    dn_a_log = jnp.log(jax.random.uniform(next(ks), (L, DN_HEADS), f32, 1.0, 16.0))
    dt = jnp.exp(jax.random.uniform(next(ks), (L, DN_HEADS), f32,
                                    float(np.log(1e-3)), float(np.log(1e-1))))
    dn_dt_bias = dt + jnp.log(-jnp.expm1(-dt))
    dn_o_norm = gain((L, DN_HEAD_DIM))
    w_br_fox = nrm((L, FOX_WIDTH, D_MODEL), FOX_WIDTH ** -0.5)
    w_br_mla = nrm((L, MLA_WIDTH, D_MODEL), MLA_WIDTH ** -0.5)
    w_br_dn = nrm((L, DN_WIDTH, D_MODEL), DN_WIDTH ** -0.5)
    w_out = nrm((L, D_MODEL, D_MODEL), D_MODEL ** -0.5)
    ffn2_norm = gain((L, D_MODEL))
    ffn2_w_gu = nrm((L, D_MODEL, 2 * D_FF), D_MODEL ** -0.5)
    ffn2_w_down = nrm((L, D_FF, D_MODEL), D_FF ** -0.5)
    final_norm = gain((D_MODEL,))
    return {'x': x, 'positions': positions,
            'ffn1_norm': ffn1_norm, 'ffn1_w_gu': ffn1_w_gu, 'ffn1_w_down': ffn1_w_down,
            'mix_norm': mix_norm, 'w_in': w_in, 'b_gate': b_gate, 'fox_b_f': fox_b_f,
            'mla_q_norm': mla_q_norm, 'mla_w_uq': mla_w_uq, 'mla_kv_norm': mla_kv_norm, 'mla_w_ukv': mla_w_ukv,
            'dn_conv_w': dn_conv_w, 'dn_a_log': dn_a_log, 'dn_dt_bias': dn_dt_bias, 'dn_o_norm': dn_o_norm,
            'w_br_fox': w_br_fox, 'w_br_mla': w_br_mla, 'w_br_dn': w_br_dn, 'w_out': w_out,
            'ffn2_norm': ffn2_norm, 'ffn2_w_gu': ffn2_w_gu, 'ffn2_w_down': ffn2_w_down,
            'final_norm': final_norm}


def reference(x, positions, ffn1_norm, ffn1_w_gu, ffn1_w_down, mix_norm, w_in, b_gate, fox_b_f,
              mla_q_norm, mla_w_uq, mla_kv_norm, mla_w_ukv, dn_conv_w, dn_a_log, dn_dt_bias, dn_o_norm,
              w_br_fox, w_br_mla, w_br_dn, w_out, ffn2_norm, ffn2_w_gu, ffn2_w_down, final_norm):
    cos, sin = rope_tables(positions)
    for l in range(DEPTH):
        x = x + 0.5 * swiglu(rmsnorm(x, ffn1_norm[l]), ffn1_w_gu[l], ffn1_w_down[l])
        x = x + hybrid_mixer(rmsnorm(x, mix_norm[l]), cos, sin, w_in[l], b_gate[l], fox_b_f[l],
                             mla_q_norm[l], mla_w_uq[l], mla_kv_norm[l], mla_w_ukv[l],
                             dn_conv_w[l], dn_a_log[l], dn_dt_bias[l], dn_o_norm[l],
                             w_br_fox[l], w_br_mla[l], w_br_dn[l], w_out[l])
        x = x + 0.5 * swiglu(rmsnorm(x, ffn2_norm[l]), ffn2_w_gu[l], ffn2_w_down[l])
    return rmsnorm(x, final_norm)
```

```python
from contextlib import ExitStack
import numpy as np
import concourse.bass as bass
import concourse.mybir as mybir
from concourse.bass_utils import run_bass_kernel_spmd

F32 = mybir.dt.float32
BF16 = mybir.dt.bfloat16
I32 = mybir.dt.int32
ALU = mybir.AluOpType
AF = mybir.ActivationFunctionType
AX = mybir.AxisListType

D = 1024
DFF = 2816
NCORE = 8
TOK = 4096
TT = 512
NTT = TOK // TT
EPS = 1e-6


class Dep:
    __slots__ = ("w", "r", "mw")

    def __init__(self):
        self.w = None
        self.r = []
        self.mw = []


class Prog:
    NDS = 8

    def __init__(self, nc, es):
        self.nc = nc
        self.es = es
        self.ses = es
        self.eng = {"pe": nc.tensor, "act": nc.scalar, "dve": nc.vector, "pool": nc.gpsimd, "sp": nc.sync}
        self.csem = {e: es.enter_context(nc.semaphore("c_" + e)) for e in self.eng}
        self.dsem = {}
        self.cnt = {e: 0 for e in self.eng}
        self.dcnt = {e: 0 for e in self.eng}
        self.seen = {e: {} for e in self.eng}
        self.nwait = 0
        self.ccnt = 0
        self.ccsem = None

    def sem(self, k):
        if k[0] == "c":
            return self.csem[k[1]]
        if k[0] == "x":
            return self.ccsem
        if k not in self.dsem:
            self.dsem[k] = self.ses.enter_context(self.nc.semaphore("d_%s_%d" % (k[1], k[2])))
        return self.dsem[k]

    def _collect(self, e, r, w, wm=()):
        need = {}
        seen = self.seen[e]

        def add(p):
            k, v = p
            if e == "pe" and k == ("c", "pe"):
                return
            if seen.get(k, 0) >= v:
                return
            if need.get(k, 0) < v:
                need[k] = v

        for d in r:
            if d.w is not None:
                add(d.w)
            for p in d.mw:
                add(p)
        for d in w:
            if d.w is not None:
                add(d.w)
            for p in d.r:
                add(p)
            for p in d.mw:
                add(p)
        for d in wm:
            if d.w is not None:
                add(d.w)
            for p in d.r:
                add(p)
        return need

    def _emit(self, e, fn, need):
        eng = self.eng[e]
        items = list(need.items())
        for k, v in items[1:]:
            eng.wait_ge(self.sem(k), v)
            self.nwait += 1
        ins = fn(eng)
        if items:
            ins._wait_ge(self.sem(items[0][0]), items[0][1])
        for k, v in items:
            self.seen[e][k] = v
        return ins

    def _commit(self, pt, r, w, wm=()):
        for d in wm:
            d.mw = [p for p in d.mw if p[0] != pt[0]]
            d.mw.append(pt)
        for d in w:
            d.w = pt
            d.r = []
            d.mw = []
        for d in r:
            d.r = [p for p in d.r if p[0] != pt[0]]
            d.r.append(pt)

    def op(self, e, fn, r=(), w=()):
        need = self._collect(e, r, w)
        ins = self._emit(e, fn, need)
        self.cnt[e] += 1
        ins.then_inc(self.csem[e], 1)
        self._commit((("c", e), self.cnt[e]), r, w)

    def dma(self, q, out, in_, r=(), w=(), wm=(), **kw):
        j = self.dcnt[q]
        slot = j % self.NDS
        key = ("d", q, slot)
        val = 16 * (j // self.NDS + 1)
        need = self._collect(q, r, w, wm)
        if j >= self.NDS and self.seen[q].get(key, 0) < val - 16:
            need[key] = max(need.get(key, 0), val - 16)
        ins = self._emit(q, lambda eng: eng.dma_start(out=out, in_=in_, **kw), need)
        ins.then_inc(self.sem(key), 16)
        self.dcnt[q] += 1
        self._commit((key, val), r, w, wm)

    def collective(self, kind, src_ap, dst_ap, r=(), w=()):
        q = "pool"
        if self.ccsem is None:
            self.ccsem = self.ses.enter_context(self.nc.semaphore("cc_sem"))
        need = self._collect(q, r, w)
        eng = self.eng[q]
        for k, v in need.items():
            eng.wait_ge(self.sem(k), v)
            self.seen[q][k] = v
            self.nwait += 1
        ins = eng.collective_compute(kind, ALU.bypass, replica_groups=[list(range(NCORE))], ins=[src_ap], outs=[dst_ap])
        ins.then_inc(self.ccsem)
        self.ccnt += 1
        self._commit((("x",), self.ccnt), r, w)

    def barrier(self):
        pts = []
        for e, n in self.cnt.items():
            if n:
                pts.append((("c", e), n))
        for q, n in self.dcnt.items():
            for slot in range(min(n, self.NDS)):
                pts.append((("d", q, slot), 16 * ((n - 1 - slot) // self.NDS + 1)))
        if self.ccnt:
            pts.append((("x",), self.ccnt))
        for e in self.eng:
            for k, v in pts:
                if k == ("c", e):
                    continue
                if self.seen[e].get(k, 0) >= v:
                    continue
                self.eng[e].wait_ge(self.sem(k), v)
                self.seen[e][k] = v
                self.nwait += 1

    class _Scope:
        def __init__(self, P):
            self.P = P

        def __enter__(self):
            self.old = self.P.es
            self.P.es = ExitStack()
            self.P.es.__enter__()
            return self

        def __exit__(self, *a):
            self.P.barrier()
            self.P.es.__exit__(*a)
            self.P.es = self.old
            return False

    def scope(self):
        return Prog._Scope(self)

    def finish(self):
        sp = self.eng["sp"]
        for q, n in self.dcnt.items():
            for slot in range(min(n, self.NDS)):
                cntslot = (n - 1 - slot) // self.NDS + 1
                sp.wait_ge(self.sem(("d", q, slot)), 16 * cntslot)
        for e, n in self.cnt.items():
            if n:
                sp.wait_ge(self.csem[e], n)


class TB:
    def __init__(self, t, nslots=1):
        self.t = t
        self.d = [Dep() for _ in range(nslots)]


_UNIQ = [0]


def sb(P, name, shape, dtype, nslots=1):
    _UNIQ[0] += 1
    return TB(P.es.enter_context(P.nc.sbuf_tensor("s_%s_%d" % (name, _UNIQ[0]), shape, dtype)), nslots)


def ps(P, name, shape=(128, 512), dtype=F32):
    return TB(P.es.enter_context(P.nc.psum_tensor("p_" + name, list(shape), dtype)), 1)


def host_consts():
    c = {}
    c["ident"] = np.eye(128, dtype=np.float32)
    c["onesD"] = np.full((128, 128), 1.0 / D, np.float32)
    c["ones256"] = np.full((128, 128), 1.0 / 256, np.float32)
    c["ones128"] = np.full((128, 128), 1.0 / 128, np.float32)
    half = 16
    invf = (np.float32(10000.0) ** (-np.arange(half, dtype=np.float32) / np.float32(half))).astype(np.float32)
    c["invf"] = np.concatenate([invf, invf]).reshape(32, 1).astype(np.float32)
    c["sgn"] = np.concatenate([-np.ones(16), np.ones(16)]).reshape(32, 1).astype(np.float32)
    return c


def load_const(P, name, dram_ap, shape, dtype=F32, q="sp"):
    t = sb(P, name, list(shape), dtype)
    P.dma(q, t.t[:], dram_ap, w=[t.d[0]])
    return t


class Stage:
    def __init__(self, P, width, nslots=2, src_dep=()):
        self.P = P
        self.src_dep = list(src_dep)
        self.tb = sb(P, "wstage", [128, nslots, width], F32, nslots)
        self.i = 0
        self.n = nslots
        self.width = width

    def load(self, dst_ap, dst_dep, src_ap, ncols, scale_ap=None, scale_dep=None, parts=128):
        P = self.P
        s = self.i % self.n
        self.i += 1
        st = self.tb.t[:parts, s, :ncols]
        q = ("sp", "act")[self.i % 2]
        P.dma(q, st, src_ap, r=self.src_dep, w=[self.tb.d[s]])
        e = ("dve", "pool")[self.i % 2]
        if scale_ap is None:
            P.op(e, lambda en: en.tensor_copy(out=dst_ap, in_=st), r=[self.tb.d[s]], w=[dst_dep])
        else:
            P.op(e, lambda en: en.tensor_scalar(dst_ap, st, scale_ap, None, ALU.mult),
                 r=[self.tb.d[s], scale_dep], w=[dst_dep])


def rms_scale(P, x_ap, xd, h_ap, hds, nk, n, ones, sq, psS, tmp, rstd, mhalf):
    P.op("dve", lambda e: e.tensor_tensor(out=sq.t[:, :nk, :n], in0=x_ap, in1=x_ap, op=ALU.mult), r=[xd], w=[sq.d[0]])
    for k in range(nk):
        P.op("pe", lambda e, k=k: e.matmul(psS.t[:, :n], ones.t[:, :], sq.t[:, k, :n], start=(k == 0), stop=(k == nk - 1)),
             r=[sq.d[0], ones.d[0]], w=[psS.d[0]])
    P.op("dve", lambda e: e.tensor_scalar(tmp.t[:, :n], psS.t[:, :n], EPS, None, ALU.add), r=[psS.d[0]], w=[tmp.d[0]])
    P.op("pool", lambda e: e.tensor_tensor(out=rstd.t[:, :n], in0=tmp.t[:, :n], in1=mhalf.t[:, :n], op=ALU.pow),
         r=[tmp.d[0], mhalf.d[0]], w=[rstd.d[0]])
    for k in range(nk):
        en = "dve" if k % 2 == 0 else "pool"
        P.op(en, lambda e, k=k: e.tensor_tensor(out=h_ap[:, k, :], in0=x_ap[:, k, :], in1=rstd.t[:, :n], op=ALU.mult),
             r=[xd, rstd.d[0]], w=[hds[k]])


FT = 256


def phase_ffn(P, C, xin, xout, gain, wgu, wdown, final_gain=None, xin_dep=(), xout_dep=(), w_dep=()):
    nc = P.nc
    n = FT
    NH = DFF // 128
    g_sb = load_const(P, "ffn_gain", gain, [128, 8])
    fg_sb = load_const(P, "ffn_fgain", final_gain, [128, 8]) if final_gain is not None else None
    wgu_sb = sb(P, "wgu", [128, 8, 2 * DFF], BF16, 32)
    wd_sb = sb(P, "wdown", [128, NH, D], BF16, NH)
    stage = Stage(P, 1408, 2, w_dep)
    for k in range(8):
        for pc in range(4):
            c0 = pc * 1408
            stage.load(wgu_sb.t[:, k, c0:c0 + 1408], wgu_sb.d[k * 4 + pc], wgu[:, k, c0:c0 + 1408], 1408,
                       scale_ap=g_sb.t[:, k:k + 1], scale_dep=g_sb.d[0])
    for j in range(NH):
        stage.load(wd_sb.t[:, j, :], wd_sb.d[j], wdown[:, j, :], 1024)
    xs = sb(P, "ffn_x", [128, 2, 8, n], F32, 2)
    sq = sb(P, "ffn_sq", [128, 8, n], BF16)
    h = sb(P, "ffn_h", [128, 8, n], BF16, 8)
    act = sb(P, "ffn_act", [128, NH, n], BF16, NH)
    sg = sb(P, "ffn_sg", [128, 2, n], F32, 2)
    tmp = sb(P, "ffn_tmp", [128, n], F32)
    rstd = sb(P, "ffn_rstd", [128, n], F32)
    psS = C["psum"][0]
    psG = [C["psum"][1], C["psum"][2]]
    psU = [C["psum"][3], C["psum"][4]]
    psD = [C["psum"][5], C["psum"][6]]
    ones, mhalf = C["onesD"], C["mhalf"]
    NT = TOK // n
    P.dma("sp", xs.t[:, 0, :, :], xin[:, :, 0:n], r=xin_dep, w=[xs.d[0]])
    for t in range(NT):
        s = t % 2
        x_ap = xs.t[:, s, :, :]
        xd = xs.d[s]
        if t + 1 < NT:
            P.dma("sp", xs.t[:, 1 - s, :, :], xin[:, :, (t + 1) * n:(t + 2) * n], r=xin_dep, w=[xs.d[1 - s]])
        rms_scale(P, x_ap, xd, h.t[:, :, :], h.d, 8, n, ones, sq, psS, tmp, rstd, mhalf)
        for j in range(NH):
            b = j % 2
            for k in range(8):
                P.op("pe", lambda e, j=j, k=k, b=b: e.matmul(psG[b].t[:, :n], wgu_sb.t[:, k, j * 128:(j + 1) * 128], h.t[:, k, :],
                                                             start=(k == 0), stop=(k == 7)),
                     r=[wgu_sb.d[k * 4 + (j * 128) // 1408], h.d[k]], w=[psG[b].d[0]])
            for k in range(8):
                P.op("pe", lambda e, j=j, k=k, b=b: e.matmul(psU[b].t[:, :n], wgu_sb.t[:, k, DFF + j * 128:DFF + (j + 1) * 128], h.t[:, k, :],
                                                             start=(k == 0), stop=(k == 7)),
                     r=[wgu_sb.d[k * 4 + (DFF + j * 128) // 1408], h.d[k]], w=[psU[b].d[0]])
            P.op("act", lambda e, b=b: e.activation(out=sg.t[:, b, :], in_=psG[b].t[:, :n], func=AF.Silu),
                 r=[psG[b].d[0]], w=[sg.d[b]])
            P.op("dve", lambda e, j=j, b=b: e.tensor_tensor(out=act.t[:, j, :], in0=sg.t[:, b, :], in1=psU[b].t[:, :n], op=ALU.mult),
                 r=[sg.d[b], psU[b].d[0]], w=[act.d[j]])
        for m in range(8):
            b = m % 2
            for j in range(NH):
                P.op("pe", lambda e, j=j, m=m, b=b: e.matmul(psD[b].t[:, :n], wd_sb.t[:, j, m * 128:(m + 1) * 128], act.t[:, j, :],
                                                             start=(j == 0), stop=(j == NH - 1)),
                     r=[wd_sb.d[j], act.d[j]], w=[psD[b].d[0]])
            P.op("dve", lambda e, m=m, b=b: e.scalar_tensor_tensor(out=x_ap[:, m, :], in0=psD[b].t[:, :n], scalar=0.5, in1=x_ap[:, m, :],
                                                                   op0=ALU.mult, op1=ALU.add),
                 r=[psD[b].d[0], xd], w=[xd])
        if final_gain is None:
            P.dma("sp", xout[:, :, t * n:(t + 1) * n], x_ap, r=[xd], wm=xout_dep)
        else:
            P.op("dve", lambda e: e.tensor_tensor(out=sq.t[:, :, :], in0=x_ap, in1=x_ap, op=ALU.mult), r=[xd], w=[sq.d[0]])
            for k in range(8):
                P.op("pe", lambda e, k=k: e.matmul(psS.t[:, :n], ones.t[:, :], sq.t[:, k, :], start=(k == 0), stop=(k == 7)),
                     r=[sq.d[0], ones.d[0]], w=[psS.d[0]])
            P.op("dve", lambda e: e.tensor_scalar(tmp.t[:, :n], psS.t[:, :n], EPS, None, ALU.add), r=[psS.d[0]], w=[tmp.d[0]])
            P.op("pool", lambda e: e.tensor_tensor(out=rstd.t[:, :n], in0=tmp.t[:, :n], in1=mhalf.t[:, :n], op=ALU.pow),
                 r=[tmp.d[0], mhalf.d[0]], w=[rstd.d[0]])
            for k in range(8):
                P.op("dve", lambda e, k=k: e.scalar_tensor_tensor(out=x_ap[:, k, :], in0=x_ap[:, k, :], scalar=fg_sb.t[:, k:k + 1],
                                                                   in1=rstd.t[:, :n], op0=ALU.mult, op1=ALU.mult),
                     r=[xd, rstd.d[0], fg_sb.d[0]], w=[xd])
            P.dma("sp", xout[:, :, t * n:(t + 1) * n], x_ap, r=[xd], wm=xout_dep)


def common_setup(P, consts):
    C = {}
    C["psum"] = [ps(P, "psb%d" % i) for i in range(8)]
    for nm in ("onesD", "ones256", "ones128"):
        if nm not in consts:
            continue
        o32 = sb(P, nm + "_f32", [128, 128], F32)
        P.dma("sp", o32.t[:], consts[nm], w=[o32.d[0]])
        ob = sb(P, nm + "_bf", [128, 128], BF16)
        P.op("dve", lambda e, ob=ob, o32=o32: e.tensor_copy(out=ob.t[:], in_=o32.t[:]), r=[o32.d[0]], w=[ob.d[0]])
        C[nm] = ob
    mh = sb(P, "mhalf", [128, 512], F32)
    P.op("pool", lambda e: e.memset(mh.t[:], -0.5), w=[mh.d[0]])
    C["mhalf"] = mh
    return C


def build_ffn(final=False):
    nc = bass.Bass("TRN2", target_bir_lowering=False)
    es = ExitStack()
    dr = lambda name, shape, kind="ExternalInput", dt=F32: nc.dram_tensor(name, list(shape), dt, kind=kind).ap()
    xin = dr("xin", [128, 8, TOK])
    gain = dr("gain", [128, 8])
    wgu = dr("wgu", [128, 8, 2 * DFF])
    wdown = dr("wdown", [128, DFF // 128, D])
    fgain = dr("fgain", [128, 8]) if final else None
    consts = {"onesD": dr("onesD", [128, 128])}
    xout = dr("xout", [128, 8, TOK], kind="ExternalOutput")
    with es:
        P = Prog(nc, es)
        C = common_setup(P, consts)
        phase_ffn(P, C, xin, xout, gain, wgu, wdown, fgain)
        P.finish()
    return nc, P


NPA = 4024
C_FQ, C_FK, C_FV, C_FF = 0, 512, 1024, 1536
C_CQ, C_CKV, C_KR = 1544, 1800, 1928
C_DQ, C_DK, C_DV, C_DZ, C_DB = 1960, 2472, 2984, 3496, 4008
TWO_PI = 6.283185307179586
CW1 = 6.28125
CW2 = TWO_PI - CW1
PI = 3.141592653589793


def rope_tables(P, C, pos_dram, ntok, invf, sgn):
    cos2 = sb(P, "rope_cos", [32, ntok], F32)
    sins = sb(P, "rope_sin", [32, ntok], F32)
    CH = 1024
    with P.scope():
        pi_t = sb(P, "rope_pi", [32, CH], I32)
        ang = sb(P, "rope_ang", [32, CH], F32)
        kf = sb(P, "rope_kf", [32, CH], F32)
        r = sb(P, "rope_r", [32, CH], F32)
        m = sb(P, "rope_m", [32, CH], F32)
        for c0 in range(0, ntok, CH):
            P.dma("sp", pi_t.t[:], pos_dram[c0:c0 + CH].partition_broadcast(32), w=[pi_t.d[0]])
            P.op("dve", lambda e: e.tensor_copy(out=ang.t[:], in_=pi_t.t[:]), r=[pi_t.d[0]], w=[ang.d[0]])
            P.op("dve", lambda e: e.tensor_scalar(ang.t[:], ang.t[:], invf.t[:32, 0:1], None, ALU.mult), r=[ang.d[0], invf.d[0]], w=[ang.d[0]])

            def reduce_sin(dst, shift):
                if shift:
                    P.op("dve", lambda e: e.tensor_scalar(kf.t[:], ang.t[:], shift, 1.0 / TWO_PI, ALU.add, ALU.mult), r=[ang.d[0]], w=[kf.d[0]])
                else:
                    P.op("dve", lambda e: e.tensor_scalar(kf.t[:], ang.t[:], 1.0 / TWO_PI, None, ALU.mult), r=[ang.d[0]], w=[kf.d[0]])
                P.op("dve", lambda e: e.tensor_copy(out=pi_t.t[:], in_=kf.t[:]), r=[kf.d[0]], w=[pi_t.d[0]])
                P.op("dve", lambda e: e.tensor_copy(out=kf.t[:], in_=pi_t.t[:]), r=[pi_t.d[0]], w=[kf.d[0]])
                P.op("dve", lambda e: e.scalar_tensor_tensor(out=r.t[:], in0=kf.t[:], scalar=-CW1, in1=ang.t[:], op0=ALU.mult, op1=ALU.add),
                     r=[kf.d[0], ang.d[0]], w=[r.d[0]])
                P.op("dve", lambda e: e.scalar_tensor_tensor(out=r.t[:], in0=kf.t[:], scalar=-CW2, in1=r.t[:], op0=ALU.mult, op1=ALU.add),
                     r=[kf.d[0], r.d[0]], w=[r.d[0]])
                if shift:
                    P.op("dve", lambda e: e.tensor_scalar(r.t[:], r.t[:], shift, None, ALU.add), r=[r.d[0]], w=[r.d[0]])
                for _ in range(2):
                    P.op("dve", lambda e: e.tensor_scalar(m.t[:], r.t[:], PI, -TWO_PI, ALU.is_gt, ALU.mult), r=[r.d[0]], w=[m.d[0]])
                    P.op("dve", lambda e: e.tensor_tensor(out=r.t[:], in0=r.t[:], in1=m.t[:], op=ALU.add), r=[r.d[0], m.d[0]], w=[r.d[0]])
                    P.op("dve", lambda e: e.tensor_scalar(m.t[:], r.t[:], -PI, TWO_PI, ALU.is_lt, ALU.mult), r=[r.d[0]], w=[m.d[0]])
                    P.op("dve", lambda e: e.tensor_tensor(out=r.t[:], in0=r.t[:], in1=m.t[:], op=ALU.add), r=[r.d[0], m.d[0]], w=[r.d[0]])
                P.op("act", lambda e: e.activation(out=dst.t[:, c0:c0 + CH], in_=r.t[:], func=AF.Sin), r=[r.d[0]], w=[dst.d[0]])

            reduce_sin(sins, 0.0)
            reduce_sin(cos2, PI / 2)
        P.op("dve", lambda e: e.tensor_scalar(sins.t[:], sins.t[:], sgn.t[:32, 0:1], None, ALU.mult), r=[sins.d[0], sgn.d[0]], w=[sins.d[0]])
    return cos2, sins


def phase_proj(P, C, xin, gain, win, qgain, wuq, kvgain, wukv, pos, cst, outs, xin_dep=(), out_dep=(), w_dep=()):
    n = TT
    g_sb = load_const(P, "pj_gain", gain, [128, 8])
    qg_sb = load_const(P, "pj_qgain", qgain, [128, 2])
    kg_sb = load_const(P, "pj_kvgain", kvgain, [128, 1])
    invf = load_const(P, "pj_invf", cst["invf"], [32, 1])
    sgn = load_const(P, "pj_sgn", cst["sgn"], [32, 1])
    ones256 = C["ones256"]
    ones128 = C["ones128"]
    NPC = 2
    PCW = NPA // NPC
    win_sb = sb(P, "win", [128, 8, NPA], BF16, 8 * NPC)
    krsw_sb = sb(P, "krsw", [128, 8, 32], BF16, 8)
    wuq_sb = sb(P, "wuq", [128, 2, 768], BF16, 2)
    wuqsw_sb = sb(P, "wuqsw", [128, 2, 8, 32], BF16, 2)
    wukv_sb = sb(P, "wukv", [128, 1024], BF16, 1)
    sc = P.scope()
    sc.__enter__()
    stage = Stage(P, PCW, 2, w_dep)
    for k in range(8):
        for pc in range(NPC):
            stage.load(win_sb.t[:, k, pc * PCW:(pc + 1) * PCW], win_sb.d[k * NPC + pc], win[:, k, pc * PCW:(pc + 1) * PCW], PCW,
                       scale_ap=g_sb.t[:, k:k + 1], scale_dep=g_sb.d[0])
    for k in range(8):
        P.op("pool", lambda e, k=k: e.tensor_copy(out=krsw_sb.t[:, k, 0:16], in_=win_sb.t[:, k, C_KR + 16:C_KR + 32]),
             r=[win_sb.d[k * NPC + (C_KR // PCW)]], w=[krsw_sb.d[k]])
        P.op("pool", lambda e, k=k: e.tensor_copy(out=krsw_sb.t[:, k, 16:32], in_=win_sb.t[:, k, C_KR:C_KR + 16]),
             r=[win_sb.d[k * NPC + (C_KR // PCW)]], w=[krsw_sb.d[k]])
    for kk in range(2):
        stage.load(wuq_sb.t[:, kk, :], wuq_sb.d[kk], wuq[:, kk, :], 768, scale_ap=qg_sb.t[:, kk:kk + 1], scale_dep=qg_sb.d[0])
        v = wuq_sb.t[:, kk, :].rearrange("p (h c) -> p h c", c=96)
        P.op("pool", lambda e, kk=kk, v=v: e.tensor_copy(out=wuqsw_sb.t[:, kk, :, 0:16], in_=v[:, :, 80:96]), r=[wuq_sb.d[kk]], w=[wuqsw_sb.d[kk]])
        P.op("pool", lambda e, kk=kk, v=v: e.tensor_copy(out=wuqsw_sb.t[:, kk, :, 16:32], in_=v[:, :, 64:80]), r=[wuq_sb.d[kk]], w=[wuqsw_sb.d[kk]])
    stage.load(wukv_sb.t[:, :], wukv_sb.d[0], wukv[:, :], 1024, scale_ap=kg_sb.t[:, 0:1], scale_dep=kg_sb.d[0])
    sc.__exit__(None, None, None)

    cos2, sins = rope_tables(P, C, pos, TOK, invf, sgn)

    xs = sb(P, "pj_x", [128, 2, 8, n], F32, 2)
    sq = sb(P, "pj_sq", [128, 8, n], BF16)
    h = sb(P, "pj_h", [128, 8, n], BF16, 8)
    tmp = sb(P, "pj_tmp", [128, n], F32)
    rstd = sb(P, "pj_rstd", [128, n], F32)
    cq = sb(P, "pj_cq", [128, 2, n], F32)
    cqn = sb(P, "pj_cqn", [128, 2, n], BF16, 2)
    ckv = sb(P, "pj_ckv", [128, 1, n], F32)
    kvn = sb(P, "pj_kvn", [128, 1, n], BF16, 1)
    NOS = 6
    ost = sb(P, "pj_ost", [128, NOS, n], F32, NOS)
    osb = sb(P, "pj_osb", [128, NOS, n], BF16, NOS)
    rt1 = sb(P, "pj_rt1", [32, n], F32)
    rt2 = sb(P, "pj_rt2", [32, n], F32)
    psS = C["psum"][0]
    psM = [C["psum"][i] for i in (1, 2, 3, 4)]
    psR = [C["psum"][5], C["psum"][6]]
    ones, mhalf = C["onesD"], C["mhalf"]
    state = {"pm": 0, "of": 0, "ob": 0, "ev": 0, "dq": 0}

    def wdeps(k, c0, c1):
        return [win_sb.d[k * NPC + pc] for pc in range(NPC) if not (c1 <= pc * PCW or c0 >= (pc + 1) * PCW)]

    def proj_chunk(c0, m):
        b = psM[state["pm"] % 4]
        state["pm"] += 1
        for k in range(8):
            P.op("pe", lambda e, k=k: e.matmul(b.t[:m, :n], win_sb.t[:, k, c0:c0 + m], h.t[:, k, :], start=(k == 0), stop=(k == 7)),
                 r=wdeps(k, c0, c0 + m) + [h.d[k]], w=[b.d[0]])
        return b

    def evac(dst_ap, dst_dep, src, m, extra_r=()):
        en = ("act", "dve")[state["ev"] % 2]
        state["ev"] += 1
        if en == "act":
            P.op("act", lambda e: e.copy(out=dst_ap, in_=src.t[:m, :n]), r=[src.d[0]] + list(extra_r), w=[dst_dep])
        else:
            P.op("dve", lambda e: e.tensor_copy(out=dst_ap, in_=src.t[:m, :n]), r=[src.d[0]] + list(extra_r), w=[dst_dep])

    def store(dram_ap, sb_ap, dep):
        q = ("sp", "pool")[state["dq"] % 2]
        state["dq"] += 1
        P.dma(q, dram_ap, sb_ap, r=[dep], wm=out_dep)

    def out_f32(src, m, parts):
        s = state["of"] % NOS
        state["of"] += 1
        evac(ost.t[:m, s, :], ost.d[s], src, m)
        for (p0, p1, dram_ap) in parts:
            store(dram_ap, ost.t[p0:p1, s, :], ost.d[s])

    def out_bf(src, m, parts):
        s = state["ob"] % NOS
        state["ob"] += 1
        evac(osb.t[:m, s, :], osb.d[s], src, m)
        for (p0, p1, dram_ap) in parts:
            store(dram_ap, osb.t[p0:p1, s, :], osb.d[s])

    def rope_out(ps_x, ps_sw, t0, dram_ap):
        s = state["ob"] % NOS
        state["ob"] += 1
        P.op("dve", lambda e: e.tensor_tensor(out=rt1.t[:, :], in0=ps_x.t[:32, :n], in1=cos2.t[:, t0:t0 + n], op=ALU.mult),
             r=[ps_x.d[0], cos2.d[0]], w=[rt1.d[0]])
        P.op("dve", lambda e: e.tensor_tensor(out=rt2.t[:, :], in0=ps_sw.t[:32, :n], in1=sins.t[:, t0:t0 + n], op=ALU.mult),
             r=[ps_sw.d[0], sins.d[0]], w=[rt2.d[0]])
        P.op("pool", lambda e: e.tensor_tensor(out=osb.t[:32, s, :], in0=rt1.t[:, :], in1=rt2.t[:, :], op=ALU.add),
             r=[rt1.d[0], rt2.d[0]], w=[osb.d[s]])
        for (p0, p1, ap_) in dram_ap:
            store(ap_, osb.t[p0:p1, s, :], osb.d[s])

    NT = TOK // n
    P.dma("sp", xs.t[:, 0, :, :], xin[:, :, 0:n], r=xin_dep, w=[xs.d[0]])
    for t in range(NT):
        s = t % 2
        t0 = t * n
        tsl = slice(t0, t0 + n)
        x_ap = xs.t[:, s, :, :]
        xd = xs.d[s]
        if t + 1 < NT:
            P.dma("sp", xs.t[:, 1 - s, :, :], xin[:, :, (t + 1) * n:(t + 2) * n], r=xin_dep, w=[xs.d[1 - s]])
        rms_scale(P, x_ap, xd, h.t[:, :, :], h.d, 8, n, ones, sq, psS, tmp, rstd, mhalf)
        for kk in range(2):
            b = proj_chunk(C_CQ + kk * 128, 128)
            evac(cq.t[:, kk, :], cq.d[0], b, 128)
        b = proj_chunk(C_CKV, 128)
        evac(ckv.t[:, 0, :], ckv.d[0], b, 128)
        for g, c0 in enumerate((C_FQ, C_FK, C_FV)):
            for ch in range(4):
                b = proj_chunk(c0 + ch * 128, 128)
                out_bf(b, 128, outs["fox"](g, ch, tsl))
        b = proj_chunk(C_FF, 8)
        out_f32(b, 8, outs["ff"](tsl))
        rms_scale(P, cq.t[:, :, :], cq.d[0], cqn.t[:, :, :], cqn.d, 2, n, ones256, sq, psS, tmp, rstd, mhalf)
        for j in range(8):
            b = psM[state["pm"] % 4]
            state["pm"] += 1
            for kk in range(2):
                P.op("pe", lambda e, kk=kk, j=j, b=b: e.matmul(b.t[:64, :n], wuq_sb.t[:, kk, 96 * j:96 * j + 64], cqn.t[:, kk, :],
                                                               start=(kk == 0), stop=(kk == 1)),
                     r=[wuq_sb.d[kk], cqn.d[kk]], w=[b.d[0]])
            out_bf(b, 64, outs["mq"](j, 0, tsl))
            for kk in range(2):
                P.op("pe", lambda e, kk=kk, j=j: e.matmul(psR[0].t[:32, :n], wuq_sb.t[:, kk, 96 * j + 64:96 * j + 96], cqn.t[:, kk, :],
                                                          start=(kk == 0), stop=(kk == 1)),
                     r=[wuq_sb.d[kk], cqn.d[kk]], w=[psR[0].d[0]])
            for kk in range(2):
                P.op("pe", lambda e, kk=kk, j=j: e.matmul(psR[1].t[:32, :n], wuqsw_sb.t[:, kk, j, :], cqn.t[:, kk, :],
                                                          start=(kk == 0), stop=(kk == 1)),
                     r=[wuqsw_sb.d[kk], cqn.d[kk]], w=[psR[1].d[0]])
            rope_out(psR[0], psR[1], t0, outs["mq"](j, 1, tsl))
        rms_scale(P, ckv.t[:, :, :], ckv.d[0], kvn.t[:, :, :], kvn.d, 1, n, ones128, sq, psS, tmp, rstd, mhalf)
        for j in range(8):
            b = psM[state["pm"] % 4]
            state["pm"] += 1
            P.op("pe", lambda e, j=j, b=b: e.matmul(b.t[:, :n], wukv_sb.t[:, 128 * j:128 * j + 128], kvn.t[:, 0, :], start=True, stop=True),
                 r=[wukv_sb.d[0], kvn.d[0]], w=[b.d[0]])
            out_bf(b, 128, outs["mkv"](j, tsl))
        for k in range(8):
            P.op("pe", lambda e, k=k: e.matmul(psR[0].t[:32, :n], win_sb.t[:, k, C_KR:C_KR + 32], h.t[:, k, :], start=(k == 0), stop=(k == 7)),
                 r=wdeps(k, C_KR, C_KR + 32) + [h.d[k]], w=[psR[0].d[0]])
        for k in range(8):
            P.op("pe", lambda e, k=k: e.matmul(psR[1].t[:32, :n], krsw_sb.t[:, k, :], h.t[:, k, :], start=(k == 0), stop=(k == 7)),
                 r=[krsw_sb.d[k], h.d[k]], w=[psR[1].d[0]])
        rope_out(psR[0], psR[1], t0, outs["mkr"](tsl))
        for g, c0 in enumerate((C_DQ, C_DK, C_DV, C_DZ)):
            for ch in range(4):
                b = proj_chunk(c0 + ch * 128, 128)
                out_bf(b, 128, outs["dn"](g, ch, tsl))
        b = proj_chunk(C_DB, 16)
        out_f32(b, 16, outs["dba"](tsl))


def build_proj():
    nc = bass.Bass("TRN2", target_bir_lowering=False)
    es = ExitStack()
    dr = lambda name, shape, kind="ExternalInput", dt=F32: nc.dram_tensor(name, list(shape), dt, kind=kind).ap()
    xin = dr("xin", [128, 8, TOK])
    gain = dr("gain", [128, 8])
    win = dr("win", [128, 8, NPA])
    qgain = dr("qgain", [128, 2])
    wuq = dr("wuq", [128, 2, 768])
    kvgain = dr("kvgain", [128, 1])
    wukv = dr("wukv", [128, 1024])
    pos = dr("pos", [TOK], dt=I32)
    consts = {k: dr(k, list(v.shape)) for k, v in host_consts().items()}
    outs = {
        "fox": dr("o_fox", [3, 512, TOK], "ExternalOutput", BF16),
        "ff": dr("o_ff", [8, TOK], "ExternalOutput"),
        "mq": dr("o_mq", [8, 96, TOK], "ExternalOutput", BF16),
        "mkv": dr("o_mkv", [8, 128, TOK], "ExternalOutput", BF16),
        "mkr": dr("o_mkr", [32, TOK], "ExternalOutput", BF16),
        "dn": dr("o_dn", [4, 512, TOK], "ExternalOutput"),
        "dba": dr("o_dba", [16, TOK], "ExternalOutput"),
    }
    with es:
        P = Prog(nc, es)
        C = common_setup(P, consts)
        phase_proj(P, C, xin, gain, win, qgain, wuq, kvgain, wukv, pos, consts, outs)
        P.finish()
    return nc, P


def h_xT(x2d):
    tok = x2d.shape[0]
    return np.ascontiguousarray(x2d.T.reshape(8, 128, tok).transpose(1, 0, 2))


def h_xT_inv(a):
    tok = a.shape[2]
    return np.ascontiguousarray(a.transpose(1, 0, 2).reshape(1024, tok).T)


def h_w(w):
    return np.ascontiguousarray(w.reshape(-1, 128, w.shape[1]).transpose(1, 0, 2))


def h_vec(v):
    return np.ascontiguousarray(v.reshape(-1, 128).T)


def core_tokens(c):
    b, q = divmod(c, 4)
    return b, q * TOK, (q + 1) * TOK


S = 16384
QB = 512
NQB = S // QB
NKB = S // 128


def mix_consts():
    c = {}
    c["ident"] = np.eye(128, dtype=np.float32)
    p = np.arange(128)[:, None]
    j = np.arange(512)[None, :]
    c["amask"] = np.stack([(r * 128 + p <= j).astype(np.float32) for r in range(4)], axis=1)
    f = np.arange(128)
    c["triu"] = (f[:, None] <= f[None, :]).astype(np.float32)
    c["trius"] = (f[:, None] < f[None, :]).astype(np.float32)
    c["ones1"] = np.ones((128, 128), np.float32)
    return c


def mix_shared(P, C, cst):
    A = {}
    idf = sb(P, "at_idf", [128, 128], F32)
    P.dma("sp", idf.t[:], cst["ident"], w=[idf.d[0]])
    A["identf"] = idf
    A["identb"] = sb(P, "at_idb", [128, 128], BF16)
    P.op("dve", lambda e: e.tensor_copy(out=A["identb"].t[:], in_=idf.t[:]), r=[idf.d[0]], w=[A["identb"].d[0]])
    A["ones1"] = load_const(P, "at_ones1", cst["ones1"], [128, 128])
    A["triu"] = load_const(P, "at_triu", cst["triu"], [128, 128])
    A["trius"] = load_const(P, "at_trius", cst["trius"], [128, 128])
    return A


def attn_setup(P, C, cst, A):
    A["kT"] = sb(P, "at_kT", [96, S], BF16, 4)
    A["v"] = sb(P, "at_v", [128, NKB, 65], BF16, 16)
    P.op("pool", lambda e: e.memset(A["v"].t[:, :, 64:65], 1.0), w=A["v"].d)
    A["q"] = sb(P, "at_q", [96, 2, QB], BF16, 2)
    A["pT"] = sb(P, "at_pT", [128, 4, QB], BF16, 4)
    A["vst"] = sb(P, "at_vst", [64, 2, 1024], BF16, 2)
    A["rec"] = sb(P, "at_rec", [65, QB], F32)
    A["bc"] = sb(P, "at_bc", [64, QB], F32)
    A["o"] = sb(P, "at_o", [64, 2, QB], BF16, 2)
    m32 = sb(P, "at_m32", [128, 4, 512], F32)
    P.dma("sp", m32.t[:], cst["amask"], w=[m32.d[0]])
    A["mask"] = sb(P, "at_mask", [128, 4, 512], BF16)
    P.op("dve", lambda e: e.tensor_copy(out=A["mask"].t[:], in_=m32.t[:]), r=[m32.d[0]], w=[A["mask"].d[0]])
    A["bias"] = sb(P, "at_bias", [128, NKB, NQB], F32)
    A["ps_s"] = [C["psum"][0], C["psum"][1], C["psum"][2]]
    A["ps_acc"] = [C["psum"][3], C["psum"][4]]
    A["ps_bc"] = C["psum"][5]
    A["ps_x"] = [C["psum"][6], C["psum"][7]]
    A["cnt"] = {"s": 0, "p": 0, "acc": 0, "x": 0, "vst": 0, "q": 0, "o": 0}
    return A


def fox_bias(P, A, ff_dram_b, bf_col):
    with P.scope():
        fr = sb(P, "fb_fr", [128, 128], F32)
        e1 = sb(P, "fb_e1", [128, 128], F32)
        l1 = sb(P, "fb_l1", [128, 128], F32)
        l1T = sb(P, "fb_l1T", [128, 128], F32)
        cp = sb(P, "fb_cp", [128, 128], F32)
        cpT = sb(P, "fb_cpT", [128, 128], F32)
        off = sb(P, "fb_off", [128, 1], F32)
        nbf = sb(P, "fb_nbf", [128, 1], F32)
        ccol = sb(P, "fb_ccol", [128, 128], F32)
        cref = sb(P, "fb_cref", [128, NQB], F32)
        px = A["ps_x"]
        for (p0, p1, ap, rd) in ff_dram_b:
            P.dma("sp", fr.t[p0:p1, :], ap, r=rd, w=[fr.d[0]])
        P.op("dve", lambda e: e.tensor_scalar(nbf.t[:], bf_col.t[:, 0:1], -1.0, None, ALU.mult), r=[bf_col.d[0]], w=[nbf.d[0]])
        P.op("act", lambda e: e.activation(out=e1.t[:], in_=fr.t[:], func=AF.Exp, scale=-1.0, bias=nbf.t[:, 0:1]),
             r=[fr.d[0], nbf.d[0]], w=[e1.d[0]])
        P.op("act", lambda e: e.activation(out=l1.t[:], in_=e1.t[:], func=AF.Ln, bias=1.0), r=[e1.d[0]], w=[l1.d[0]])
        P.op("pe", lambda e: e.matmul(px[0].t[:, :128], l1.t[:, :], A["identf"].t[:, :], start=True, stop=True),
             r=[l1.d[0], A["identf"].d[0]], w=[px[0].d[0]])
        P.op("dve", lambda e: e.tensor_copy(out=l1T.t[:], in_=px[0].t[:, :128]), r=[px[0].d[0]], w=[l1T.d[0]])
        P.op("pe", lambda e: e.matmul(px[1].t[:, :128], l1T.t[:, :], A["triu"].t[:, :], start=True, stop=True),
             r=[l1T.d[0], A["triu"].d[0]], w=[px[1].d[0]])
        P.op("dve", lambda e: e.tensor_copy(out=cp.t[:], in_=px[1].t[:, :128]), r=[px[1].d[0]], w=[cp.d[0]])
        P.op("pe", lambda e: e.matmul(px[0].t[:, 0:1], A["trius"].t[:, :], cp.t[:, 127:128], start=True, stop=True),
             r=[cp.d[0], A["trius"].d[0]], w=[px[0].d[0]])
        P.op("dve", lambda e: e.tensor_copy(out=off.t[:], in_=px[0].t[:, 0:1]), r=[px[0].d[0]], w=[off.d[0]])
        P.op("dve", lambda e: e.tensor_scalar(cp.t[:], cp.t[:], off.t[:, 0:1], None, ALU.add), r=[cp.d[0], off.d[0]], w=[cp.d[0]])
        P.op("pe", lambda e: e.matmul(px[1].t[:, :128], cp.t[:, :], A["identf"].t[:, :], start=True, stop=True),
             r=[cp.d[0], A["identf"].d[0]], w=[px[1].d[0]])
        P.op("dve", lambda e: e.tensor_copy(out=cpT.t[:], in_=px[1].t[:, :128]), r=[px[1].d[0]], w=[cpT.d[0]])
        P.op("dve", lambda e: e.tensor_scalar(ccol.t[:], A["ones1"].t[:], cp.t[:, 127:128], None, ALU.mult),
             r=[cp.d[0], A["ones1"].d[0]], w=[ccol.d[0]])
        sel = A["identf"].t[:, :].rearrange("p (q r) -> p q r", r=4)[:, :, 3]
        P.op("pe", lambda e: e.matmul(px[0].t[:, :NQB], ccol.t[:, :], sel, start=True, stop=True),
             r=[ccol.d[0], A["identf"].d[0]], w=[px[0].d[0]])
        P.op("dve", lambda e: e.tensor_copy(out=cref.t[:], in_=px[0].t[:, :NQB]), r=[px[0].d[0]], w=[cref.d[0]])
        P.op("dve", lambda e: e.tensor_tensor(out=A["bias"].t[:, :, :],
                                              in0=cpT.t[:, :].unsqueeze(2).to_broadcast([128, NKB, NQB]),
                                              in1=cref.t[:, :].unsqueeze(1).to_broadcast([128, NKB, NQB]), op=ALU.subtract),
             r=[cpT.d[0], cref.d[0]], w=[A["bias"].d[0]])


def attention(P, A, qT, kT, vT, oT, dq, scale, use_bias):
    cn = A["cnt"]
    kT_sb, v_sb = A["kT"], A["v"]
    for i in range(4):
        for (r0, r1, ap, rd) in kT(i * 4096, 4096):
            P.dma("sp", kT_sb.t[r0:r1, i * 4096:(i + 1) * 4096], ap, r=rd, w=[kT_sb.d[i]])
    for pc in range(16):
        s = cn["vst"] % 2
        cn["vst"] += 1
        for (r0, r1, ap, rd) in vT(pc * 1024, 1024):
            P.dma("sp", A["vst"].t[r0:r1, s, :], ap, r=rd, w=[A["vst"].d[s]])
        px = A["ps_x"][cn["x"] % 2]
        cn["x"] += 1
        for j in range(8):
            P.op("pe", lambda e, j=j, s=s, px=px: e.matmul(px.t[:, j * 64:(j + 1) * 64], A["vst"].t[:, s, j * 128:(j + 1) * 128],
                                                           A["identb"].t[:64, :64], start=True, stop=True),
                 r=[A["vst"].d[s], A["identb"].d[0]], w=[px.d[0]])
        P.op("dve", lambda e, pc=pc, px=px: e.tensor_copy(out=v_sb.t[:, pc * 8:(pc + 1) * 8, 0:64],
                                                         in_=px.t[:, :512].rearrange("p (j d) -> p j d", d=64)),
             r=[px.d[0]], w=[v_sb.d[pc]])
    LA = 2

    def load_q(qb):
        s = cn["q"] % 2
        cn["q"] += 1
        for (r0, r1, ap, rd) in qT(qb * QB, QB):
            P.dma("sp", A["q"].t[r0:r1, s, :], ap, r=rd, w=[A["q"].d[s]])
        return s

    qs_next = load_q(0)
    for qb in range(NQB):
        qs = qs_next
        if qb + 1 < NQB:
            qs_next = load_q(qb + 1)
        nkb = 4 * (qb + 1)
        acc = A["ps_acc"][cn["acc"] % 2]
        cn["acc"] += 1
        pslots = {}
        for i in range(nkb + LA):
            if i < nkb:
                kb = i
                sc = A["ps_s"][cn["s"] % 3]
                cn["s"] += 1
                P.op("pe", lambda e, kb=kb, sc=sc: e.matmul(sc.t[:, :QB], kT_sb.t[:dq, kb * 128:(kb + 1) * 128], A["q"].t[:dq, qs, :],
                                                            start=True, stop=True),
                     r=[kT_sb.d[kb // 32], A["q"].d[qs]], w=[sc.d[0]])
                ps_ = cn["p"] % 4
                cn["p"] += 1
                pslots[kb] = ps_
                if use_bias:
                    P.op("act", lambda e, kb=kb, sc=sc, ps_=ps_: e.activation(out=A["pT"].t[:, ps_, :], in_=sc.t[:, :QB], func=AF.Exp,
                                                                              scale=scale, bias=A["bias"].t[:, kb, qb:qb + 1]),
                         r=[sc.d[0], A["bias"].d[0]], w=[A["pT"].d[ps_]])
                else:
                    P.op("act", lambda e, sc=sc, ps_=ps_: e.activation(out=A["pT"].t[:, ps_, :], in_=sc.t[:, :QB], func=AF.Exp, scale=scale),
                         r=[sc.d[0]], w=[A["pT"].d[ps_]])
                if kb >= 4 * qb:
                    rr = kb - 4 * qb
                    P.op("pool", lambda e, ps_=ps_, rr=rr: e.tensor_tensor(out=A["pT"].t[:, ps_, :], in0=A["pT"].t[:, ps_, :],
                                                                           in1=A["mask"].t[:, rr, :], op=ALU.mult),
                         r=[A["pT"].d[ps_], A["mask"].d[0]], w=[A["pT"].d[ps_]])
            j = i - LA
            if j >= 0:
                ps_ = pslots[j]
                P.op("pe", lambda e, j=j, ps_=ps_, acc=acc: e.matmul(acc.t[:65, :QB], v_sb.t[:, j, :], A["pT"].t[:, ps_, :],
                                                                     start=(j == 0), stop=(j == nkb - 1)),
                     r=[v_sb.d[j // 8], A["pT"].d[ps_]], w=[acc.d[0]])
        P.op("dve", lambda e, acc=acc: e.reciprocal(out=A["rec"].t[64:65, :], in_=acc.t[64:65, :QB]), r=[acc.d[0]], w=[A["rec"].d[0]])
        P.op("pe", lambda e: e.matmul(A["ps_bc"].t[:64, :QB], A["ones1"].t[64:65, 0:64], A["rec"].t[64:65, :], start=True, stop=True),
             r=[A["rec"].d[0], A["ones1"].d[0]], w=[A["ps_bc"].d[0]])
        P.op("act", lambda e: e.copy(out=A["bc"].t[:, :], in_=A["ps_bc"].t[:64, :QB]), r=[A["ps_bc"].d[0]], w=[A["bc"].d[0]])
        os_ = cn["o"] % 2
        cn["o"] += 1
        P.op("dve", lambda e, acc=acc, os_=os_: e.tensor_tensor(out=A["o"].t[:, os_, :], in0=acc.t[:64, :QB], in1=A["bc"].t[:, :], op=ALU.mult),
             r=[acc.d[0], A["bc"].d[0]], w=[A["o"].d[os_]])
        P.dma("pool", oT[0][:, qb * QB:(qb + 1) * QB], A["o"].t[:, os_, :], r=[A["o"].d[os_]], w=[oT[1]])


GC = 64
GG = 512
NG = S // GG


def gdn_consts():
    c = {}
    j = np.arange(64)[:, None]
    i = np.arange(64)[None, :]
    c["maskneg"] = np.where(i >= j, 0.0, -30000.0).astype(np.float32)
    c["sgt"] = (i > j).astype(np.float32)
    f = np.arange(128)
    c["bdtri"] = ((f[:, None] <= f[None, :]) & ((f[:, None] // 64) == (f[None, :] // 64))).astype(np.float32)
    c["ones64m"] = np.full((64, 64), 1.0 / 64, np.float32)
    c["ones64"] = np.ones((64, 64), np.float32)
    return c


def gdn_setup(P, C, cst, A):
    G = {}
    for nm, shp in (("maskneg", [64, 64]), ("sgt", [64, 64]), ("bdtri", [128, 128])):
        G[nm] = load_const(P, "g_" + nm, cst[nm], shp)
    for nm in ("ones64m", "ones64"):
        t32 = load_const(P, "g_" + nm + "f", cst[nm], [64, 64])
        tb = sb(P, "g_" + nm + "b", [64, 64], BF16)
        P.op("dve", lambda e, tb=tb, t32=t32: e.tensor_copy(out=tb.t[:], in_=t32.t[:]), r=[t32.d[0]], w=[tb.d[0]])
        G[nm] = tb
    G["identf"] = A["identf"]
    G["identb"] = A["identb"]
    G["ones1"] = A["ones1"]
    mh = sb(P, "g_mhalf", [64, 512], F32)
    P.op("pool", lambda e: e.memset(mh.t[:], -0.5), w=[mh.d[0]])
    G["mhalf"] = mh
    return G


def gdn_prep(P, G, B, I, b, scr, px):
    with P.scope():
        braw = sb(P, "gp_braw", [128, 128], F32)
        araw = sb(P, "gp_araw", [128, 128], F32)
        beta = sb(P, "gp_beta", [128, 128], F32)
        t1 = sb(P, "gp_t1", [128, 128], F32)
        g = sb(P, "gp_g", [128, 128], F32)
        gT = sb(P, "gp_gT", [128, 128], F32)
        gcp = sb(P, "gp_gcp", [128, 128], F32)
        egp = sb(P, "gp_egp", [128, 128], F32)
        na = sb(P, "gp_na", [128, 1], F32)
        glb = sb(P, "gp_glb", [128, 2, 64], F32)
        for (p0, p1, ap, rd) in I["db"](b):
            P.dma("sp", braw.t[p0:p1, :], ap, r=rd, w=[braw.d[0]])
        for (p0, p1, ap, rd) in I["da"](b):
            P.dma("sp", araw.t[p0:p1, :], ap, r=rd, w=[araw.d[0]])
        P.op("act", lambda e: e.activation(out=beta.t[:], in_=braw.t[:], func=AF.Sigmoid), r=[braw.d[0]], w=[beta.d[0]])
        P.op("act", lambda e: e.activation(out=t1.t[:], in_=araw.t[:], func=AF.Exp, bias=B["dtb"].t[:, 0:1]), r=[araw.d[0], B["dtb"].d[0]], w=[t1.d[0]])
        P.op("act", lambda e: e.activation(out=t1.t[:], in_=t1.t[:], func=AF.Ln, bias=1.0), r=[t1.d[0]], w=[t1.d[0]])
        P.op("act", lambda e: e.activation(out=na.t[:], in_=B["alog"].t[:, 0:1], func=AF.Exp), r=[B["alog"].d[0]], w=[na.d[0]])
        P.op("dve", lambda e: e.tensor_scalar(g.t[:], t1.t[:], na.t[:, 0:1], -1.0, ALU.mult, ALU.mult), r=[t1.d[0], na.d[0]], w=[g.d[0]])
        P.op("pe", lambda e: e.matmul(px[0].t[:, :128], g.t[:, :], G["identf"].t[:, :], start=True, stop=True), r=[g.d[0], G["identf"].d[0]], w=[px[0].d[0]])
        P.op("dve", lambda e: e.tensor_copy(out=gT.t[:], in_=px[0].t[:, :128]), r=[px[0].d[0]], w=[gT.d[0]])
        P.op("pe", lambda e: e.matmul(px[1].t[:, :128], gT.t[:, :], G["bdtri"].t[:, :], start=True, stop=True), r=[gT.d[0], G["bdtri"].d[0]], w=[px[1].d[0]])
        P.op("dve", lambda e: e.tensor_copy(out=gcp.t[:], in_=px[1].t[:, :128]), r=[px[1].d[0]], w=[gcp.d[0]])
        P.op("act", lambda e: e.activation(out=egp.t[:], in_=gcp.t[:], func=AF.Exp), r=[gcp.d[0]], w=[egp.d[0]])
        P.dma("sp", scr["beta"][b], beta.t[:], r=[beta.d[0]], w=[scr["d_beta"][b]])
        P.dma("sp", scr["gc"][b], gcp.t[:], r=[gcp.d[0]], w=[scr["d_gc"][b]])
        P.dma("sp", scr["egc"][b], egp.t[:], r=[egp.d[0]], w=[scr["d_egc"][b]])
        for h in range(2):
            P.op("pe", lambda e, h=h: e.matmul(px[0].t[:64, :128], gcp.t[:, h * 64:(h + 1) * 64], G["identf"].t[:, :], start=True, stop=True),
                 r=[gcp.d[0], G["identf"].d[0]], w=[px[0].d[0]])
            P.op("dve", lambda e, h=h: e.tensor_copy(out=B["gc_col"].t[:, :, h], in_=px[0].t[:64, :128]), r=[px[0].d[0]], w=[B["gc_col"].d[0]])
            P.op("pe", lambda e, h=h: e.matmul(px[1].t[:64, :128], beta.t[:, h * 64:(h + 1) * 64], G["identf"].t[:, :], start=True, stop=True),
                 r=[beta.d[0], G["identf"].d[0]], w=[px[1].d[0]])
            P.op("dve", lambda e, h=h: e.tensor_copy(out=B["beta_col"].t[:, :, h], in_=px[1].t[:64, :128]), r=[px[1].d[0]], w=[B["beta_col"].d[0]])
            P.op("dve", lambda e, h=h: e.tensor_scalar(glb.t[:, h, :], G["ones1"].t[:, :64], gcp.t[:, h * 64 + 63:h * 64 + 64], None, ALU.mult),
                 r=[gcp.d[0], G["ones1"].d[0]], w=[glb.d[0]])
            P.op("pe", lambda e, h=h: e.matmul(px[0].t[:64, :128], glb.t[:, h, :], G["identf"].t[:, :], start=True, stop=True),
                 r=[glb.d[0], G["identf"].d[0]], w=[px[0].d[0]])
            P.op("dve", lambda e, h=h: e.tensor_copy(out=B["gl_col"].t[:, :, h], in_=px[0].t[:64, :128]), r=[px[0].d[0]], w=[B["gl_col"].d[0]])
        fl = lambda tb: tb.t[:, :, :].rearrange("t p h -> t (p h)")
        P.op("act", lambda e: e.activation(out=fl(B["kbg_s"]), in_=fl(B["gc_col"]), func=AF.Exp), r=[B["gc_col"].d[0]], w=[B["kbg_s"].d[0]])
        P.op("dve", lambda e: e.tensor_tensor(out=fl(B["kbg_s"]), in0=fl(B["kbg_s"]), in1=fl(B["beta_col"]), op=ALU.mult),
             r=[B["kbg_s"].d[0], B["beta_col"].d[0]], w=[B["kbg_s"].d[0]])
        P.op("dve", lambda e: e.tensor_tensor(out=fl(B["kdec_s"]), in0=fl(B["gl_col"]), in1=fl(B["gc_col"]), op=ALU.subtract),
             r=[B["gl_col"].d[0], B["gc_col"].d[0]], w=[B["kdec_s"].d[0]])
        P.op("act", lambda e: e.activation(out=fl(B["kdec_s"]), in_=fl(B["kdec_s"]), func=AF.Exp), r=[B["kdec_s"].d[0]], w=[B["kdec_s"].d[0]])
        P.op("act", lambda e: e.activation(out=fl(B["glast_e"]), in_=fl(B["gl_col"]), func=AF.Exp), r=[B["gl_col"].d[0]], w=[B["glast_e"].d[0]])


def gdn_alloc(P, b):
    B = {}
    f32 = lambda nm, shp, ns=1: sb(P, "gd%d_%s" % (b, nm), shp, F32, ns)
    bf = lambda nm, shp, ns=1: sb(P, "gd%d_%s" % (b, nm), shp, BF16, ns)
    for nm in ("gc_col", "beta_col", "gl_col", "kbg_s", "kdec_s", "glast_e"):
        B[nm] = f32(nm, [64, 128, 2])
    B["x"] = f32("x", [64, 3, GG + 3], 3)
    B["z"] = f32("z", [64, GG])
    B["bbc"] = f32("bbc", [64, GG]); B["gbc"] = f32("gbc", [64, GG]); B["ebc"] = f32("ebc", [64, GG])
    B["cv"] = f32("cv", [64, 3, GG], 3)
    B["sq"] = bf("sq", [64, 2, GG], 2)
    B["tmp"] = f32("tmp", [64, 2, GG], 2)
    B["rstd"] = f32("rstd", [64, 2, GG], 2)
    B["qn"] = f32("qn", [64, GG]); B["kn"] = f32("kn", [64, GG])
    B["kT"] = bf("kT", [64, GG]); B["qT"] = bf("qT", [64, GG]); B["kbT"] = bf("kbT", [64, GG]); B["qdT"] = bf("qdT", [64, GG])
    B["kbg"] = bf("kbg", [64, 8, 64]); B["kdec"] = bf("kdec", [64, 8, 64]); B["vb"] = bf("vb", [64, 8, 64])
    B["diff"] = f32("diff", [64, 8, 64]); B["DT"] = f32("DT", [64, 8, 64]); B["DTs"] = f32("DTs", [64, 8, 64])
    B["Nf"] = f32("Nf", [64, 8, 64]); B["Af"] = f32("Af", [64, 8, 64])
    B["Pb"] = bf("Pb", [64, 8, 64]); B["Ptb"] = bf("Ptb", [64, 8, 64]); B["Ab"] = bf("Ab", [64, 8, 64])
    B["QKm"] = bf("QKm", [64, 8, 64])
    B["wT"] = bf("wT", [64, 8, 64]); B["u"] = f32("u", [64, 8, 64])
    B["vnew"] = bf("vnew", [64, 64])
    B["Sf"] = f32("Sf", [64, 64]); B["Sb"] = bf("Sb", [64, 64])
    B["of"] = f32("of", [64, GG]); B["osq"] = bf("osq", [64, GG]); B["zs"] = f32("zs", [64, GG]); B["y"] = f32("y", [64, GG])
    B["yo"] = bf("yo", [64, GG])
    return B


def gdn_group(P, G, B, I, O, b, g, scr, X, XO):
    t0 = g * GG
    c0 = g * 8
    bc8 = lambda tb: tb.t[:, :, :].rearrange("t p h -> t (p h)")[:, c0:c0 + 8].unsqueeze(2).to_broadcast([64, 8, 64])
    v3 = lambda ap: ap.rearrange("p (c i) -> p c i", i=64)
    for gi in range(3):
        if g == 0:
            P.op("pool", lambda e, gi=gi: e.memset(B["x"].t[:, gi, 0:3], 0.0), w=[B["x"].d[gi]])
            for (c0_, n_, ap, rd) in I["dn"](b, gi, 0, GG):
                P.dma("pool", B["x"].t[:, gi, 3 + c0_:3 + c0_ + n_], ap, r=rd, w=[B["x"].d[gi]])
        else:
            for (c0_, n_, ap, rd) in I["dn"](b, gi, t0 - 3, GG + 3):
                P.dma("pool", B["x"].t[:, gi, c0_:c0_ + n_], ap, r=rd, w=[B["x"].d[gi]])
    for (c0_, n_, ap, rd) in I["dn"](b, 3, t0, GG):
        P.dma("pool", B["z"].t[:, c0_:c0_ + n_], ap, r=rd, w=[B["z"].d[0]])
    fl = lambda name: scr[name][b].rearrange("p f -> (p f)")[t0:t0 + GG].partition_broadcast(64)
    P.dma("sp", B["bbc"].t[:, :], fl("beta"), r=[scr["d_beta"][b]], w=[B["bbc"].d[0]])
    P.dma("sp", B["gbc"].t[:, :], fl("gc"), r=[scr["d_gc"][b]], w=[B["gbc"].d[0]])
    P.dma("sp", B["ebc"].t[:, :], fl("egc"), r=[scr["d_egc"][b]], w=[B["ebc"].d[0]])
    for gi in range(3):
        xw = B["x"]
        P.op("pool", lambda e, gi=gi: e.tensor_scalar(B["cv"].t[:, gi, :], xw.t[:, gi, 0:GG], B["cw"].t[:, gi, 0:1], None, ALU.mult),
             r=[xw.d[gi], B["cw"].d[0]], w=[B["cv"].d[gi]])
        for i in range(1, 4):
            P.op("dve", lambda e, gi=gi, i=i: e.scalar_tensor_tensor(out=B["cv"].t[:, gi, :], in0=xw.t[:, gi, i:i + GG], scalar=B["cw"].t[:, gi, i:i + 1],
                                                                      in1=B["cv"].t[:, gi, :], op0=ALU.mult, op1=ALU.add),
                 r=[xw.d[gi], B["cw"].d[0], B["cv"].d[gi]], w=[B["cv"].d[gi]])
        P.op("act", lambda e, gi=gi: e.activation(out=B["cv"].t[:, gi, :], in_=B["cv"].t[:, gi, :], func=AF.Silu), r=[B["cv"].d[gi]], w=[B["cv"].d[gi]])
    yield
    for gi in range(2):
        P.op("pool", lambda e, gi=gi: e.tensor_tensor(out=B["sq"].t[:, gi, :], in0=B["cv"].t[:, gi, :], in1=B["cv"].t[:, gi, :], op=ALU.mult),
             r=[B["cv"].d[gi]], w=[B["sq"].d[gi]])
        P.op("pe", lambda e, gi=gi: e.matmul(X[gi].t[:64, :GG], G["ones64"].t[:, :], B["sq"].t[:, gi, :], start=True, stop=True),
             r=[B["sq"].d[gi], G["ones64"].d[0]], w=[X[gi].d[0]])
        P.op("dve", lambda e, gi=gi: e.tensor_scalar(B["tmp"].t[:, gi, :], X[gi].t[:64, :GG], EPS, None, ALU.add), r=[X[gi].d[0]], w=[B["tmp"].d[gi]])
        P.op("pool", lambda e, gi=gi: e.tensor_tensor(out=B["rstd"].t[:, gi, :], in0=B["tmp"].t[:, gi, :], in1=G["mhalf"].t[:, :], op=ALU.pow),
             r=[B["tmp"].d[gi], G["mhalf"].d[0]], w=[B["rstd"].d[gi]])
    P.op("dve", lambda e: e.scalar_tensor_tensor(out=B["qn"].t[:, :], in0=B["cv"].t[:, 0, :], scalar=0.125, in1=B["rstd"].t[:, 0, :], op0=ALU.mult, op1=ALU.mult),
         r=[B["cv"].d[0], B["rstd"].d[0]], w=[B["qn"].d[0]])
    P.op("dve", lambda e: e.tensor_tensor(out=B["kn"].t[:, :], in0=B["cv"].t[:, 1, :], in1=B["rstd"].t[:, 1, :], op=ALU.mult),
         r=[B["cv"].d[1], B["rstd"].d[1]], w=[B["kn"].d[0]])
    P.op("act", lambda e: e.copy(out=B["kT"].t[:, :], in_=B["kn"].t[:, :]), r=[B["kn"].d[0]], w=[B["kT"].d[0]])
    P.op("act", lambda e: e.copy(out=B["qT"].t[:, :], in_=B["qn"].t[:, :]), r=[B["qn"].d[0]], w=[B["qT"].d[0]])
    P.op("dve", lambda e: e.tensor_tensor(out=B["kbT"].t[:, :], in0=B["kn"].t[:, :], in1=B["bbc"].t[:, :], op=ALU.mult),
         r=[B["kn"].d[0], B["bbc"].d[0]], w=[B["kbT"].d[0]])
    P.op("pool", lambda e: e.tensor_tensor(out=B["qdT"].t[:, :], in0=B["qn"].t[:, :], in1=B["ebc"].t[:, :], op=ALU.mult),
         r=[B["qn"].d[0], B["ebc"].d[0]], w=[B["qdT"].d[0]])
    P.op("dve", lambda e: e.tensor_tensor(out=B["diff"].t[:, :, :], in0=v3(B["gbc"].t[:, :]), in1=bc8(B["gc_col"]), op=ALU.subtract),
         r=[B["gbc"].d[0], B["gc_col"].d[0]], w=[B["diff"].d[0]])
    P.op("dve", lambda e: e.tensor_tensor(out=B["diff"].t[:, :, :], in0=B["diff"].t[:, :, :],
                                          in1=G["maskneg"].t[:, :].unsqueeze(1).to_broadcast([64, 8, 64]), op=ALU.add),
         r=[B["diff"].d[0], G["maskneg"].d[0]], w=[B["diff"].d[0]])
    P.op("act", lambda e: e.activation(out=B["DT"].t[:, :, :], in_=B["diff"].t[:, :, :], func=AF.Exp), r=[B["diff"].d[0]], w=[B["DT"].d[0]])
    P.op("pool", lambda e: e.tensor_tensor(out=B["DTs"].t[:, :, :], in0=B["DT"].t[:, :, :],
                                           in1=G["sgt"].t[:, :].unsqueeze(1).to_broadcast([64, 8, 64]), op=ALU.mult),
         r=[B["DT"].d[0], G["sgt"].d[0]], w=[B["DTs"].d[0]])
    yield
    for c in range(8):
        P.op("pe", lambda e, c=c: e.matmul(X[0].t[:64, c * 64:(c + 1) * 64], B["kn"].t[:, c * 64:(c + 1) * 64], G["identf"].t[:64, :64], start=True, stop=True),
             r=[B["kn"].d[0], G["identf"].d[0]], w=[X[0].d[0]])
    for c in range(8):
        P.op("pe", lambda e, c=c: e.matmul(X[1].t[:64, c * 64:(c + 1) * 64], B["cv"].t[:, 2, c * 64:(c + 1) * 64], G["identf"].t[:64, :64], start=True, stop=True),
             r=[B["cv"].d[2], G["identf"].d[0]], w=[X[1].d[0]])
    yield
    P.op("dve", lambda e: e.tensor_tensor(out=B["kbg"].t[:, :, :], in0=v3(X[0].t[:64, :GG]), in1=bc8(B["kbg_s"]), op=ALU.mult),
         r=[X[0].d[0], B["kbg_s"].d[0]], w=[B["kbg"].d[0]])
    P.op("dve", lambda e: e.tensor_tensor(out=B["kdec"].t[:, :, :], in0=v3(X[0].t[:64, :GG]), in1=bc8(B["kdec_s"]), op=ALU.mult),
         r=[X[0].d[0], B["kdec_s"].d[0]], w=[B["kdec"].d[0]])
    P.op("dve", lambda e: e.tensor_tensor(out=B["vb"].t[:, :, :], in0=v3(X[1].t[:64, :GG]), in1=bc8(B["beta_col"]), op=ALU.mult),
         r=[X[1].d[0], B["beta_col"].d[0]], w=[B["vb"].d[0]])
    for c in range(8):
        cb = slice(c * 64, (c + 1) * 64)
        P.op("pe", lambda e, cb=cb: e.matmul(X[2].t[:64, cb], B["kT"].t[:, cb], B["kbT"].t[:, cb], start=True, stop=True),
             r=[B["kT"].d[0], B["kbT"].d[0]], w=[X[2].d[0]])
    yield
    P.op("dve", lambda e: e.tensor_tensor(out=B["Nf"].t[:, :, :], in0=v3(X[2].t[:64, :GG]), in1=B["DTs"].t[:, :, :], op=ALU.mult),
         r=[X[2].d[0], B["DTs"].d[0]], w=[B["Nf"].d[0]])
    yield
    for c in range(8):
        cb = slice(c * 64, (c + 1) * 64)
        P.op("pe", lambda e, cb=cb: e.matmul(X[0].t[:64, cb], B["kT"].t[:, cb], B["qT"].t[:, cb], start=True, stop=True),
             r=[B["kT"].d[0], B["qT"].d[0]], w=[X[0].d[0]])
    P.op("dve", lambda e: e.tensor_tensor(out=B["QKm"].t[:, :, :], in0=v3(X[0].t[:64, :GG]), in1=B["DT"].t[:, :, :], op=ALU.mult),
         r=[X[0].d[0], B["DT"].d[0]], w=[B["QKm"].d[0]])
    P.op("act", lambda e: e.copy(out=B["Pb"].t[:, :, :], in_=B["Nf"].t[:, :, :]), r=[B["Nf"].d[0]], w=[B["Pb"].d[0]])
    P.op("pool", lambda e: e.tensor_tensor(out=B["Af"].t[:, :, :], in0=G["identf"].t[:64, :64].unsqueeze(1).to_broadcast([64, 8, 64]),
                                           in1=B["Nf"].t[:, :, :], op=ALU.subtract),
         r=[B["Nf"].d[0], G["identf"].d[0]], w=[B["Af"].d[0]])
    P.op("act", lambda e: e.copy(out=B["Ab"].t[:, :, :], in_=B["Af"].t[:, :, :]), r=[B["Af"].d[0]], w=[B["Ab"].d[0]])
    for c in range(8):
        P.op("pe", lambda e, c=c: e.matmul(X[1].t[:64, c * 64:(c + 1) * 64], B["Pb"].t[:, c, :], G["identb"].t[:64, :64], start=True, stop=True),
             r=[B["Pb"].d[0], G["identb"].d[0]], w=[X[1].d[0]])
    P.op("act", lambda e: e.copy(out=B["Ptb"].t[:, :, :], in_=v3(X[1].t[:64, :GG])), r=[X[1].d[0]], w=[B["Ptb"].d[0]])
    yield
    for it in range(5):
        last = it == 4
        if not last:
            for c in range(8):
                P.op("pe", lambda e, c=c: e.matmul(X[2].t[:64, c * 64:(c + 1) * 64], B["Ptb"].t[:, c, :], B["Pb"].t[:, c, :], start=True, stop=True),
                     r=[B["Ptb"].d[0], B["Pb"].d[0]], w=[X[2].d[0]])
        for c in range(8):
            P.op("pe", lambda e, c=c: e.matmul(X[1].t[:64, c * 64:(c + 1) * 64], B["Pb"].t[:, c, :], B["Ptb"].t[:, c, :], start=True, stop=True),
                 r=[B["Ptb"].d[0], B["Pb"].d[0]], w=[X[1].d[0]])
        yield
        P.op("act", lambda e: e.copy(out=B["Ptb"].t[:, :, :], in_=v3(X[1].t[:64, :GG])), r=[X[1].d[0]], w=[B["Ptb"].d[0]])
        if not last:
            P.op("dve", lambda e: e.tensor_copy(out=B["Pb"].t[:, :, :], in_=v3(X[2].t[:64, :GG])), r=[X[2].d[0]], w=[B["Pb"].d[0]])
        yield
        for c in range(8):
            P.op("pe", lambda e, c=c: e.matmul(X[0].t[:64, c * 64:(c + 1) * 64], B["Ptb"].t[:, c, :], B["Ab"].t[:, c, :], start=True, stop=True),
                 r=[B["Ptb"].d[0], B["Ab"].d[0]], w=[X[0].d[0]])
        yield
        P.op("dve", lambda e: e.tensor_tensor(out=B["Af"].t[:, :, :], in0=v3(X[0].t[:64, :GG]), in1=B["Af"].t[:, :, :], op=ALU.add),
             r=[X[0].d[0], B["Af"].d[0]], w=[B["Af"].d[0]])
        P.op("act", lambda e: e.copy(out=B["Ab"].t[:, :, :], in_=B["Af"].t[:, :, :]), r=[B["Af"].d[0]], w=[B["Ab"].d[0]])
        yield
    for c in range(8):
        P.op("pe", lambda e, c=c: e.matmul(X[1].t[:64, c * 64:(c + 1) * 64], B["kbg"].t[:, c, :], B["Ab"].t[:, c, :], start=True, stop=True),
             r=[B["kbg"].d[0], B["Ab"].d[0]], w=[X[1].d[0]])
    for c in range(8):
        P.op("pe", lambda e, c=c: e.matmul(X[2].t[:64, c * 64:(c + 1) * 64], B["Ab"].t[:, c, :], B["vb"].t[:, c, :], start=True, stop=True),
             r=[B["vb"].d[0], B["Ab"].d[0]], w=[X[2].d[0]])
    yield
    P.op("act", lambda e: e.copy(out=B["wT"].t[:, :, :], in_=v3(X[1].t[:64, :GG])), r=[X[1].d[0]], w=[B["wT"].d[0]])
    P.op("dve", lambda e: e.tensor_copy(out=B["u"].t[:, :, :], in_=v3(X[2].t[:64, :GG])), r=[X[2].d[0]], w=[B["u"].d[0]])
    yield
    s1, s1d, s2, s2d = X[0].t[:64, 0:64], X[0].d[0], X[1].t[:64, 0:64], X[1].d[0]
    for c in range(8):
        cb = slice(c * 64, (c + 1) * 64)
        P.op("pe", lambda e, c=c: e.matmul(s1, B["wT"].t[:, c, :], B["Sb"].t[:, :], start=True, stop=True),
             r=[B["wT"].d[0], B["Sb"].d[0]], w=[s1d])
        yield
        P.op("dve", lambda e, c=c: e.tensor_tensor(out=B["vnew"].t[:, :], in0=B["u"].t[:, c, :], in1=s1, op=ALU.subtract),
             r=[B["u"].d[0], s1d], w=[B["vnew"].d[0]])
        yield
        P.op("pe", lambda e, cb=cb: e.matmul(XO.t[:64, cb], B["Sb"].t[:, :], B["qdT"].t[:, cb], start=True, stop=False),
             r=[B["Sb"].d[0], B["qdT"].d[0]], w=[XO.d[0]])
        P.op("pe", lambda e, c=c, cb=cb: e.matmul(XO.t[:64, cb], B["vnew"].t[:, :], B["QKm"].t[:, c, :], start=False, stop=True),
             r=[B["vnew"].d[0], B["QKm"].d[0]], w=[XO.d[0]])
        P.op("pe", lambda e, c=c: e.matmul(s2, B["kdec"].t[:, c, :], B["vnew"].t[:, :], start=True, stop=True),
             r=[B["kdec"].d[0], B["vnew"].d[0]], w=[s2d])
        yield
        gle = B["glast_e"].t[:, :, :].rearrange("t p h -> t (p h)")[:, c0 + c:c0 + c + 1]
        P.op("dve", lambda e, gle=gle: e.scalar_tensor_tensor(out=B["Sf"].t[:, :], in0=B["Sf"].t[:, :], scalar=gle, in1=s2, op0=ALU.mult, op1=ALU.add),
             r=[B["Sf"].d[0], B["glast_e"].d[0], s2d], w=[B["Sf"].d[0]])
        P.op("act", lambda e: e.copy(out=B["Sb"].t[:, :], in_=B["Sf"].t[:, :]), r=[B["Sf"].d[0]], w=[B["Sb"].d[0]])
        yield
    P.op("act", lambda e: e.copy(out=B["of"].t[:, :], in_=XO.t[:64, :GG]), r=[XO.d[0]], w=[B["of"].d[0]])
    P.op("pool", lambda e: e.tensor_tensor(out=B["osq"].t[:, :], in0=B["of"].t[:, :], in1=B["of"].t[:, :], op=ALU.mult), r=[B["of"].d[0]], w=[B["osq"].d[0]])
    P.op("pe", lambda e: e.matmul(X[0].t[:64, :GG], G["ones64m"].t[:, :], B["osq"].t[:, :], start=True, stop=True),
         r=[B["osq"].d[0], G["ones64m"].d[0]], w=[X[0].d[0]])
    P.op("dve", lambda e: e.tensor_scalar(B["tmp"].t[:, 0, :], X[0].t[:64, :GG], EPS, None, ALU.add), r=[X[0].d[0]], w=[B["tmp"].d[0]])
    P.op("pool", lambda e: e.tensor_tensor(out=B["rstd"].t[:, 0, :], in0=B["tmp"].t[:, 0, :], in1=G["mhalf"].t[:, :], op=ALU.pow),
         r=[B["tmp"].d[0], G["mhalf"].d[0]], w=[B["rstd"].d[0]])
    P.op("act", lambda e: e.activation(out=B["zs"].t[:, :], in_=B["z"].t[:, :], func=AF.Silu), r=[B["z"].d[0]], w=[B["zs"].d[0]])
    P.op("dve", lambda e: e.tensor_tensor(out=B["y"].t[:, :], in0=B["of"].t[:, :], in1=B["rstd"].t[:, 0, :], op=ALU.mult),
         r=[B["of"].d[0], B["rstd"].d[0]], w=[B["y"].d[0]])
    P.op("dve", lambda e: e.scalar_tensor_tensor(out=B["yo"].t[:, :], in0=B["y"].t[:, :], scalar=B["og"].t[:, 0:1], in1=B["zs"].t[:, :], op0=ALU.mult, op1=ALU.mult),
         r=[B["y"].d[0], B["og"].d[0], B["zs"].d[0]], w=[B["yo"].d[0]])
    P.dma("pool", O["dn"][0][b, :, t0:t0 + GG], B["yo"].t[:, :], r=[B["yo"].d[0]], w=[O["dn"][1]])
    yield


def gdn_all(P, C, A, G, I, O, scr):
    Bs = [gdn_alloc(P, b) for b in range(2)]
    px = [C["psum"][6], C["psum"][7]]
    for b in range(2):
        B = Bs[b]
        B["cw"] = load_const(P, "gd_cw%d" % b, I["dcw"], [64, 3, 4])
        B["dtb"] = load_const(P, "gd_dtb%d" % b, I["ddtb"], [128, 1])
        B["alog"] = load_const(P, "gd_alog%d" % b, I["dalog"], [128, 1])
        B["og"] = load_const(P, "gd_og%d" % b, I["dog"], [64, 1])
        gdn_prep(P, G, B, I, b, scr, px)
        P.op("pool", lambda e, B=B: e.memset(B["Sf"].t[:, :], 0.0), w=[B["Sf"].d[0]])
        P.op("pool", lambda e, B=B: e.memset(B["Sb"].t[:, :], 0.0), w=[B["Sb"].d[0]])
    for g in range(NG):
        gens = []
        for b in range(2):
            X = [C["psum"][4 * b + 0], C["psum"][4 * b + 1], C["psum"][4 * b + 2]]
            XO = C["psum"][4 * b + 3]
            gens.append(gdn_group(P, G, Bs[b], I, O, b, g, scr, X, XO))
        alive = [True, True]
        while any(alive):
            for b in range(2):
                if alive[b]:
                    try:
                        next(gens[b])
                    except StopIteration:
                        alive[b] = False


def phase_comb(P, C, xin, xout, gain, wg, bgate, osrc, wbf, wbm, wbd, wout, xin_dep=(), xout_dep=(), w_dep=()):
    n = TT
    g_sb = load_const(P, "cb_gain", gain, [128, 8])
    bg_sb = load_const(P, "cb_bgate", bgate, [128, 24])
    wg_sb = sb(P, "cb_wg", [128, 8, 3072], BF16, 16)
    wbr_sb = [sb(P, "cb_wbr%d" % i, [128, 4, 1024], BF16, 4) for i in range(3)]
    wo_sb = sb(P, "cb_wo", [128, 8, 1024], BF16, 8)
    with P.scope():
        stage = Stage(P, 1536, 2, w_dep)
        for k in range(8):
            for pc in range(2):
                stage.load(wg_sb.t[:, k, pc * 1536:(pc + 1) * 1536], wg_sb.d[k * 2 + pc], wg[:, k, pc * 1536:(pc + 1) * 1536], 1536,
                           scale_ap=g_sb.t[:, k:k + 1], scale_dep=g_sb.d[0])
        for i, w in enumerate((wbf, wbm, wbd)):
            for kc in range(4):
                stage.load(wbr_sb[i].t[:, kc, :], wbr_sb[i].d[kc], w[:, kc, :], 1024)
        for k in range(8):
            stage.load(wo_sb.t[:, k, :], wo_sb.d[k], wout[:, k, :], 1024)
    xs = sb(P, "cb_x", [128, 2, 8, n], F32, 2)
    os_ = [sb(P, "cb_o%d" % i, [128, 2, 4, n], BF16, 2) for i in range(3)]
    sq = sb(P, "cb_sq", [128, 8, n], BF16)
    h = sb(P, "cb_h", [128, 8, n], BF16, 8)
    tmp = sb(P, "cb_tmp", [128, n], F32)
    rstd = sb(P, "cb_rstd", [128, n], F32)
    sg = sb(P, "cb_sg", [128, 2, n], F32, 2)
    mix = sb(P, "cb_mix", [128, 2, n], F32, 2)
    tt = sb(P, "cb_tt", [128, 2, n], F32, 2)
    mixed = sb(P, "cb_mixed", [128, 8, n], BF16, 8)
    psS = C["psum"][0]
    psG = [C["psum"][1], C["psum"][2]]
    psY = [C["psum"][3], C["psum"][4]]
    psO = [C["psum"][5], C["psum"][6]]
    ones, mhalf = C["onesD"], C["mhalf"]
    NT = TOK // n

    def loads(t):
        s = t % 2
        P.dma("sp", xs.t[:, s, :, :], xin[:, :, t * n:(t + 1) * n], r=xin_dep, w=[xs.d[s]])
        for i in range(3):
            for hd in range(8):
                ap, rd = osrc(i, hd, t * n, n)
                P.dma("sp", os_[i].t[(hd % 2) * 64:(hd % 2) * 64 + 64, s, hd // 2, :], ap, r=rd, w=[os_[i].d[s]])

    loads(0)
    cg = 0
    for t in range(NT):
        s = t % 2
        x_ap = xs.t[:, s, :, :]
        xd = xs.d[s]
        if t + 1 < NT:
            loads(t + 1)
        rms_scale(P, x_ap, xd, h.t[:, :, :], h.d, 8, n, ones, sq, psS, tmp, rstd, mhalf)
        for m in range(8):
            ms = m % 2
            for br in range(3):
                b = cg % 2
                cg += 1
                c0 = br * 1024 + m * 128
                for k in range(8):
                    P.op("pe", lambda e, k=k, b=b, c0=c0: e.matmul(psG[b].t[:, :n], wg_sb.t[:, k, c0:c0 + 128], h.t[:, k, :], start=(k == 0), stop=(k == 7)),
                         r=[wg_sb.d[k * 2 + c0 // 1536], h.d[k]], w=[psG[b].d[0]])
                P.op("act", lambda e, b=b, br=br, m=m: e.activation(out=sg.t[:, b, :], in_=psG[b].t[:, :n], func=AF.Sigmoid,
                                                                   bias=bg_sb.t[:, br * 8 + m:br * 8 + m + 1]),
                     r=[psG[b].d[0], bg_sb.d[0]], w=[sg.d[b]])
                for kc in range(4):
                    P.op("pe", lambda e, kc=kc, b=b, br=br, m=m: e.matmul(psY[b].t[:, :n], wbr_sb[br].t[:, kc, m * 128:(m + 1) * 128], os_[br].t[:, s, kc, :],
                                                                          start=(kc == 0), stop=(kc == 3)),
                         r=[wbr_sb[br].d[kc], os_[br].d[s]], w=[psY[b].d[0]])
                if br == 0:
                    P.op("dve", lambda e, b=b, ms=ms: e.tensor_tensor(out=mix.t[:, ms, :], in0=sg.t[:, b, :], in1=psY[b].t[:, :n], op=ALU.mult),
                         r=[sg.d[b], psY[b].d[0]], w=[mix.d[ms]])
                elif br == 1:
                    P.op("dve", lambda e, b=b: e.tensor_tensor(out=tt.t[:, b, :], in0=sg.t[:, b, :], in1=psY[b].t[:, :n], op=ALU.mult),
                         r=[sg.d[b], psY[b].d[0]], w=[tt.d[b]])
                    P.op("pool", lambda e, b=b, ms=ms: e.tensor_tensor(out=mix.t[:, ms, :], in0=mix.t[:, ms, :], in1=tt.t[:, b, :], op=ALU.add),
                         r=[mix.d[ms], tt.d[b]], w=[mix.d[ms]])
                else:
                    P.op("dve", lambda e, b=b: e.tensor_tensor(out=tt.t[:, b, :], in0=sg.t[:, b, :], in1=psY[b].t[:, :n], op=ALU.mult),
                         r=[sg.d[b], psY[b].d[0]], w=[tt.d[b]])
                    P.op("pool", lambda e, b=b, ms=ms, m=m: e.tensor_tensor(out=mixed.t[:, m, :], in0=mix.t[:, ms, :], in1=tt.t[:, b, :], op=ALU.add),
                         r=[mix.d[ms], tt.d[b]], w=[mixed.d[m]])
        for m2 in range(8):
            b = m2 % 2
            for m in range(8):
                P.op("pe", lambda e, m=m, m2=m2, b=b: e.matmul(psO[b].t[:, :n], wo_sb.t[:, m, m2 * 128:(m2 + 1) * 128], mixed.t[:, m, :],
                                                               start=(m == 0), stop=(m == 7)),
                     r=[wo_sb.d[m], mixed.d[m]], w=[psO[b].d[0]])
            P.op("dve", lambda e, m2=m2, b=b: e.tensor_tensor(out=x_ap[:, m2, :], in0=psO[b].t[:, :n], in1=x_ap[:, m2, :], op=ALU.add),
                 r=[psO[b].d[0], xd], w=[xd])
        P.dma("sp", xout[:, :, t * n:(t + 1) * n], x_ap, r=[xd], wm=xout_dep)


NL = 4
WSEG = (("wgu1", 8 * 2 * DFF), ("wd1", 22 * D), ("win", 8 * 7096), ("wbf", 4 * D), ("wbm", 4 * D), ("wbd", 4 * D),
        ("wout", 8 * D), ("wgu2", 8 * 2 * DFF), ("wd2", 22 * D))
SROWS = {"fox": 1536, "mq": 768, "mkv": 1024, "mkr": 32, "dn": 2048, "sc": 24}
SDT = {"fox": BF16, "mq": BF16, "mkv": BF16, "mkr": BF16, "dn": BF16, "sc": F32}


def phase_mix(P, C, I, O, cst, scr, do_fox=True, do_mla=True, do_gdn=True):
    A = mix_shared(P, C, cst)
    if do_fox or do_mla:
        with P.scope():
            attn_setup(P, C, cst, A)
            bf_col = load_const(P, "bf_col", I["bf"], [128, 1])
            for b in range(2):
                if do_fox:
                    fox_bias(P, A, I["ff"](b), bf_col)
                    attention(P, A, I["fq"](b), I["fk"](b), I["fv"](b), (O["fox"][0][b], O["fox"][1]), 64, 0.125, True)
                if do_mla:
                    attention(P, A, I["mq"](b), I["mk"](b), I["mv"](b), (O["mla"][0][b], O["mla"][1]), 96, 96 ** -0.5, False)
    if do_gdn:
        with P.scope():
            G = gdn_setup(P, C, cst, A)
            gdn_all(P, C, A, G, I, O, scr)


HR = 672
RALL = 8 * HR + 32


def build_fused(nlayers=NL):
    nc = bass.Bass("TRN2", target_bir_lowering=False)
    es = ExitStack()
    dr = lambda name, shape, kind="ExternalInput", dt=F32: nc.dram_tensor(name, list(shape), dt, kind=kind).ap()
    dih = lambda name, shape, dt=F32: nc.dram_tensor(name, list(shape), dt)
    di = lambda name, shape, dt=F32: dih(name, shape, dt).ap()
    L = nlayers
    xin = dr("xin", [128, 8, TOK])
    pos = dr("pos", [TOK], dt=I32)
    wsh = {nm: dr("wsh_" + nm, [L, 16, F]) for nm, F in WSEG}
    wuq = dr("wuq", [L, 128, 2, 768])
    wukv = dr("wukv", [L, 128, 1024])
    g1 = dr("g_ffn1", [L, 128, 8]); gm = dr("g_mix", [L, 128, 8]); g2 = dr("g_ffn2", [L, 128, 8]); gf = dr("g_final", [128, 8])
    gq = dr("g_q", [L, 128, 2]); gkv = dr("g_kv", [L, 128, 1]); bgate = dr("bgate", [L, 128, 24])
    bfh = dr("bf", [L, 128, 1]); dcw = dr("dcw", [L, 64, 3, 4]); ddtb = dr("ddtb", [L, 128, 1]); dalog = dr("dalog", [L, 128, 1]); dog = dr("dog", [L, 64, 1])
    allc = dict(host_consts()); allc.update(mix_consts()); allc.update(gdn_consts())
    cst = {k: dr("c_" + k, list(v.shape)) for k, v in allc.items()}
    xout = dr("xout", [128, 8, TOK], kind="ExternalOutput")
    wsrc = {(l, nm): di("wsrc_%s_%d" % (nm, l), [16, F]) for l in range(L) for nm, F in WSEG}
    wful = {(l, nm): di("wful_%s_%d" % (nm, l), [128, F]) for l in range(L) for nm, F in WSEG}
    wdep = {(l, nm): Dep() for l in range(L) for nm, F in WSEG}
    xA, xB = di("xA", [128, 8, TOK]), di("xB", [128, 8, TOK])
    xAd, xBd = Dep(), Dep()
    HT = TOK // 2
    S_hs = [di("S_all%d" % i, [RALL, HT], BF16) for i in range(2)]; S_alld = Dep()
    G_hh = [dih("G_all%d" % i, [8 * RALL, HT], BF16) for i in range(2)]; G_hs = [h.ap() for h in G_hh]; G_alld = Dep()

    class _SA:
        def __getitem__(self, key):
            rows, tsl = key
            i = tsl.start // HT
            return S_hs[i][rows, tsl.start - i * HT:tsl.stop - i * HT]
    S_all = _SA()
    M_all = di("M_all", [8, HR + 32, TOK], BF16); M_alld = Dep()
    S_sc = di("S_sc", [24, TOK]); S_scd = Dep()
    G_sch = dih("G_sc", [8 * 24, TOK]); G_sc = G_sch.ap(); G_scd = Dep()
    M_sc = di("M_sc", [8, 3, TOK]); M_scd = Dep()
    So = di("So", [384, S], BF16); Sod = Dep()
    Goh = dih("Go", [8 * 384, S], BF16); Go = Goh.ap(); God = Dep()
    M_o = di("M_o", [8, 3, 64, TOK], BF16); M_od = Dep()
    scr = {}
    for nm in ("beta", "gc", "egc"):
        scr[nm] = di("scr_" + nm, [2, 128, 128])
        scr["d_" + nm] = [Dep(), Dep()]
    with es:
        P = Prog(nc, es)
        C = common_setup(P, cst)
        pid = P.eng["sp"].partition_id()
        bv = pid // 4
        qv = pid % 4
        for l in range(L):
            for nm, F in WSEG:
                d0 = Dep()
                P.dma("pool", wsrc[(l, nm)], wsh[nm][l], w=[d0])
                P.collective("AllGather", wsrc[(l, nm)].opt(), wful[(l, nm)].opt(), r=[d0], w=[wdep[(l, nm)]])
        cur, curd = xin, None
        nxt = [(xA, xAd), (xB, xBd)]
        ni = 0
        for l in range(L):
            W = lambda nm, k: wful[(l, nm)].rearrange("p (k n) -> p k n", k=k)
            o, od = nxt[ni % 2]; ni += 1
            with P.scope():
                phase_ffn(P, C, cur, o, g1[l], W("wgu1", 8), W("wd1", 22), None,
                          xin_dep=[curd] if curd else [], xout_dep=[od], w_dep=[wdep[(l, "wgu1")], wdep[(l, "wd1")]])
            cur, curd = o, od
            hb = lambda j: HR * j
            outs = {
                "fox": lambda g, ch, tsl: [(0, 64, S_all[hb(2 * ch) + g * 64:hb(2 * ch) + g * 64 + 64, tsl]),
                                           (64, 128, S_all[hb(2 * ch + 1) + g * 64:hb(2 * ch + 1) + g * 64 + 64, tsl])],
                "ff": lambda tsl: [(j, j + 1, S_sc[3 * j:3 * j + 1, tsl]) for j in range(8)],
                "dba": lambda tsl: [(j, j + 1, S_sc[3 * j + 1:3 * j + 2, tsl]) for j in range(8)] +
                                   [(8 + j, 9 + j, S_sc[3 * j + 2:3 * j + 3, tsl]) for j in range(8)],
                "mq": lambda j, part, tsl: [(0, 64, S_all[hb(j) + 192:hb(j) + 256, tsl])] if part == 0 else [(0, 32, S_all[hb(j) + 256:hb(j) + 288, tsl])],
                "mkv": lambda j, tsl: [(0, 128, S_all[hb(j) + 288:hb(j) + 416, tsl])],
                "mkr": lambda tsl: [(0, 32, S_all[8 * HR:8 * HR + 32, tsl])],
                "dn": lambda g, ch, tsl: [(0, 64, S_all[hb(2 * ch) + 416 + g * 64:hb(2 * ch) + 480 + g * 64, tsl]),
                                          (64, 128, S_all[hb(2 * ch + 1) + 416 + g * 64:hb(2 * ch + 1) + 480 + g * 64, tsl])],
            }
            with P.scope():
                phase_proj(P, C, cur, gm[l], W("win", 8), gq[l], wuq[l], gkv[l], wukv[l], pos, cst, outs,
                           xin_dep=[curd], out_dep=[S_alld, S_scd], w_dep=[wdep[(l, "win")]])
            for i in range(2):
                P.collective("AllGather", S_hs[i].opt(), G_hs[i].opt(), r=[S_alld], w=[G_alld])
            P.collective("AllGather", S_sc.opt(), G_sc.opt(), r=[S_scd], w=[G_scd])
            for i in range(2):
                src = bass.AP(G_hh[i], pid * (HR * HT), [[RALL * HT, 8], [HT, HR], [1, HT]])
                P.dma("sp", M_all[:, 0:HR, i * HT:(i + 1) * HT], src, r=[G_alld], wm=[M_alld])
                P.dma("sp", M_all[:, HR:HR + 32, i * HT:(i + 1) * HT], G_hs[i].rearrange("(r a) t -> r a t", r=8)[:, 8 * HR:8 * HR + 32, :],
                      r=[G_alld], wm=[M_alld])
            src = bass.AP(G_sch, pid * (3 * TOK), [[24 * TOK, 8], [1, 3 * TOK]])
            P.dma("sp", M_sc.rearrange("r k t -> r (k t)"), src, r=[G_scd], w=[M_scd])

            def seq(r0, nrows, b):
                def f(tok0, n):
                    rank = 4 * b + tok0 // TOK
                    c0 = tok0 % TOK
                    return [(0, nrows, M_all[rank, r0:r0 + nrows, c0:c0 + n], [M_alld])]
                return f

            def mk(b):
                def f(tok0, n):
                    rank = 4 * b + tok0 // TOK
                    c0 = tok0 % TOK
                    return [(0, 64, M_all[rank, 288:352, c0:c0 + n], [M_alld]),
                            (64, 96, M_all[rank, HR:HR + 32, c0:c0 + n], [M_alld])]
                return f

            def scl(k):
                def f(b):
                    return [(32 * i, 32 * i + 32, M_sc[4 * b + i, k:k + 1, :].rearrange("o (p f) -> (o p) f", f=128), [M_scd]) for i in range(4)]
                return f

            def dnsrc(b, gi, tok0, n):
                res = []
                t = tok0
                while t < tok0 + n:
                    rank = 4 * b + t // TOK
                    c0 = t % TOK
                    m = min(tok0 + n - t, TOK - c0)
                    res.append((t - tok0, m, M_all[rank, 416 + gi * 64:480 + gi * 64, c0:c0 + m], [M_alld]))
                    t += m
                return res

            I = {"fq": lambda b: seq(0, 64, b), "fk": lambda b: seq(64, 64, b), "fv": lambda b: seq(128, 64, b), "ff": scl(0), "bf": bfh[l],
                 "mq": lambda b: seq(192, 96, b), "mk": mk, "mv": lambda b: seq(352, 64, b),
                 "dn": dnsrc, "db": scl(1), "da": scl(2), "dcw": dcw[l], "ddtb": ddtb[l], "dalog": dalog[l], "dog": dog[l]}
            So3 = So.rearrange("(r b d) s -> r b d s", r=3, b=2)
            O = {"fox": (So3[0], Sod), "mla": (So3[1], Sod), "dn": (So3[2], Sod)}
            with P.scope():
                phase_mix(P, C, I, O, cst, scr)
            P.collective("AllGather", So.opt(), Go.opt(), r=[Sod], w=[God])
            src = bass.AP(Goh, bv * (64 * S) + qv * TOK, [[384 * S, 8], [128 * S, 3], [S, 64], [1, TOK]])
            P.dma("sp", M_o, src, r=[God], w=[M_od])
            def osrc(br, hd, tok0, n):
                return M_o[hd, br, :, tok0:tok0 + n], [M_od]
            o, od = nxt[ni % 2]; ni += 1
            with P.scope():
                phase_comb(P, C, cur, o, gm[l], W("win", 8)[:, :, NPA:], bgate[l], osrc, W("wbf", 4), W("wbm", 4), W("wbd", 4), W("wout", 8),
                           xin_dep=[curd], xout_dep=[od],
                           w_dep=[wdep[(l, nm)] for nm in ("win", "wbf", "wbm", "wbd", "wout")])
            cur, curd = o, od
            last = l == L - 1
            if last:
                o, od = xout, Dep()
            else:
                o, od = nxt[ni % 2]; ni += 1
            with P.scope():
                phase_ffn(P, C, cur, o, g2[l], W("wgu2", 8), W("wd2", 22), gf if last else None,
                          xin_dep=[curd], xout_dep=[od], w_dep=[wdep[(l, "wgu2")], wdep[(l, "wd2")]])
            cur, curd = o, od
        P.finish()
    return nc, P


_FUSED = {}


def kernel(x, positions, ffn1_norm, ffn1_w_gu, ffn1_w_down, mix_norm, w_in, b_gate, fox_b_f,
           mla_q_norm, mla_w_uq, mla_kv_norm, mla_w_ukv, dn_conv_w, dn_a_log, dn_dt_bias, dn_o_norm,
           w_br_fox, w_br_mla, w_br_dn, w_out, ffn2_norm, ffn2_w_gu, ffn2_w_down, final_norm):
    f32 = np.float32
    A = lambda a: np.asarray(a, f32)
    x = A(x)
    positions = np.asarray(positions, np.int32)
    L = int(np.asarray(ffn1_norm).shape[0])
    if L not in _FUSED:
        _FUSED[L] = build_fused(L)[0]
    nc = _FUSED[L]
    allc = dict(host_consts()); allc.update(mix_consts()); allc.update(gdn_consts())
    stack = lambda fn, arr: np.ascontiguousarray(np.stack([fn(A(arr[l])) for l in range(L)], axis=0))
    base = {"g_ffn1": stack(h_vec, ffn1_norm), "g_mix": stack(h_vec, mix_norm), "g_ffn2": stack(h_vec, ffn2_norm),
            "g_final": h_vec(A(final_norm)), "g_q": stack(h_vec, mla_q_norm), "g_kv": stack(h_vec, mla_kv_norm),
            "bgate": stack(h_vec, b_gate), "wuq": stack(h_w, mla_w_uq), "wukv": np.ascontiguousarray(A(mla_w_ukv)),
            "dog": np.ascontiguousarray(A(dn_o_norm).reshape(L, 64, 1))}
    for k, v in allc.items():
        base["c_" + k] = v
    wl = {"wgu1": ffn1_w_gu, "wd1": ffn1_w_down, "win": w_in, "wbf": w_br_fox, "wbm": w_br_mla, "wbd": w_br_dn,
          "wout": w_out, "wgu2": ffn2_w_gu, "wd2": ffn2_w_down}
    wfull = {nm: np.stack([h_w(A(wl[nm][l])).reshape(128, -1) for l in range(L)], axis=0) for nm, _ in WSEG}
    cw = A(dn_conv_w)
    maps = []
    for c in range(NCORE):
        b, t0, t1 = core_tokens(c)
        m = dict(base)
        m["xin"] = h_xT(x[b, t0:t1, :])
        m["pos"] = np.ascontiguousarray(positions[b, t0:t1])
        for nm, _ in WSEG:
            m["wsh_" + nm] = np.ascontiguousarray(wfull[nm][:, 16 * c:16 * c + 16, :])
        j = c
        m["bf"] = np.ascontiguousarray(np.broadcast_to(A(fox_b_f)[:, j].reshape(L, 1, 1), (L, 128, 1)))
        m["ddtb"] = np.ascontiguousarray(np.broadcast_to(A(dn_dt_bias)[:, j].reshape(L, 1, 1), (L, 128, 1)))
        m["dalog"] = np.ascontiguousarray(np.broadcast_to(A(dn_a_log)[:, j].reshape(L, 1, 1), (L, 128, 1)))
        m["dcw"] = np.ascontiguousarray(np.stack([np.stack([cw[l][:, g * 512 + 64 * j:g * 512 + 64 * j + 64].T for g in range(3)], axis=1)
                                                  for l in range(L)], axis=0))
        maps.append(m)
    res = run_bass_kernel_spmd(nc, maps, core_ids=list(range(NCORE)))
    out = np.empty((2, S, D), f32)
    for c in range(NCORE):
        b, t0, t1 = core_tokens(c)
        out[b, t0:t1, :] = h_xT_inv(np.asarray(res.results[c]["xout"]))
    return out
```
